# Optimizing a Trainium2 kernel written in Bass

```python
import math
import jax, jax.numpy as jnp
from jax import lax
import numpy as np

D_MODEL = 1024
BATCH = 4
SEQ = 4096
DEPTH = 1

CONV_WIDTH = D_MODEL // 2
CONV_K = 3
DIFF_HEADS = 4
DIFF_HEAD_DIM = 64
DIFF_V_DIM = 2 * DIFF_HEAD_DIM
DIFF_WIDTH = DIFF_HEADS * DIFF_V_DIM
N_BRANCHES = 2
IN_COLS = 3 * CONV_WIDTH + 3 * DIFF_WIDTH + N_BRANCHES * D_MODEL
D_FF = 4 * D_MODEL
Q_BLOCK = 128
RMS_EPS = 1e-6

kernel_name = "hybrid_shortconv_diffattn_gated_merge"


def rmsnorm(x, g):
    xf = x.astype(jnp.float32)
    y = xf * lax.rsqrt(jnp.mean(xf * xf, axis=-1, keepdims=True) + RMS_EPS)
    return (y * g.astype(jnp.float32)).astype(x.dtype)


def alibi_slopes(n_heads):
    return np.array([2.0 ** (-8.0 * (h + 1) / n_heads) for h in range(n_heads)], dtype=np.float32)


def lambda_init_fn(layer_idx):
    return 0.8 - 0.6 * math.exp(-0.3 * layer_idx)


def short_gated_conv(b_gate, c_gate, v, conv_w):
    h = c_gate * v
    hp = jnp.pad(h, ((0, 0), (CONV_K - 1, 0), (0, 0)))
    S = h.shape[1]
    conv = (conv_w[0] * hp[:, 0:S] + conv_w[1] * hp[:, 1:S + 1] + conv_w[2] * hp[:, 2:S + 2])
    return b_gate * conv


def diff_attention(q, k, v, lam, slopes):
    Bsz, S, H = q.shape[0], q.shape[1], q.shape[2]
    nb = S // Q_BLOCK
    scale = DIFF_HEAD_DIM ** -0.5
    q_blocks = q.reshape(Bsz, nb, Q_BLOCK, H, 2, DIFF_HEAD_DIM).transpose(1, 0, 2, 3, 4, 5)
    starts = jnp.arange(nb, dtype=jnp.int32) * Q_BLOCK
    kpos = jnp.arange(S, dtype=jnp.int32)
    slopes_f = jnp.asarray(slopes, dtype=jnp.float32)

    def one_block(args):
        qb, start = args
        s = jnp.einsum('bqhcd,bkhcd->bhcqk', qb, k).astype(jnp.float32) * scale
        qpos = start + jnp.arange(Q_BLOCK, dtype=jnp.int32)
        dist = (qpos[:, None] - kpos[None, :])
        bias = -slopes_f[:, None, None, None] * dist.astype(jnp.float32)[None, None]
        s = jnp.where(dist >= 0, s + bias, -jnp.inf)
        p = jax.nn.softmax(s, axis=-1)
        a = p[:, :, 0] - lam * p[:, :, 1]
        return jnp.einsum('bhqk,bkhv->bqhv', a.astype(v.dtype), v)

    o = lax.map(one_block, (q_blocks, starts))
    return o.transpose(1, 0, 2, 3, 4).reshape(Bsz, S, H, DIFF_V_DIM)


def setup_inputs(seed: int = 0) -> dict:
    key = jax.random.key(seed)
    ks = jax.random.split(key, 20)
    f32 = jnp.float32
    nrm = lambda k, shape, s: (jax.random.normal(k, shape, f32) * s)
    gain = lambda k, shape: 1.0 + 0.05 * jax.random.normal(k, shape, f32)
    return {
        "x": jax.random.normal(ks[0], (BATCH, SEQ, D_MODEL), f32),
        "norm_mix_g": gain(ks[1], (DEPTH, D_MODEL)),
        "w_in": nrm(ks[2], (DEPTH, D_MODEL, IN_COLS), D_MODEL ** -0.5),
        "b_gate": nrm(ks[3], (DEPTH, N_BRANCHES * D_MODEL), 0.02),
        "conv_w": nrm(ks[4], (DEPTH, CONV_K, CONV_WIDTH), CONV_K ** -0.5),
        "lambda_q1": nrm(ks[5], (DEPTH, DIFF_HEAD_DIM), 0.1),
        "lambda_k1": nrm(ks[6], (DEPTH, DIFF_HEAD_DIM), 0.1),
        "lambda_q2": nrm(ks[7], (DEPTH, DIFF_HEAD_DIM), 0.1),
        "lambda_k2": nrm(ks[8], (DEPTH, DIFF_HEAD_DIM), 0.1),
        "subln_g": gain(ks[9], (DEPTH, DIFF_V_DIM)),
        "w_a_out": nrm(ks[10], (DEPTH, CONV_WIDTH, D_MODEL), CONV_WIDTH ** -0.5),
        "w_b_out": nrm(ks[11], (DEPTH, DIFF_WIDTH, D_MODEL), DIFF_WIDTH ** -0.5),
        "w_o": nrm(ks[12], (DEPTH, D_MODEL, D_MODEL), D_MODEL ** -0.5),
        "norm_mlp_g": gain(ks[13], (DEPTH, D_MODEL)),
        "w_mlp_in": nrm(ks[14], (DEPTH, D_MODEL, D_FF), D_MODEL ** -0.5),
        "w_mlp_out": nrm(ks[15], (DEPTH, D_FF, D_MODEL), D_FF ** -0.5),
        "norm_final_g": gain(ks[16], (D_MODEL,)),
    }


def reference(x, norm_mix_g, w_in, b_gate, conv_w, lambda_q1, lambda_k1, lambda_q2, lambda_k2,
              subln_g, w_a_out, w_b_out, w_o, norm_mlp_g, w_mlp_in, w_mlp_out, norm_final_g):
    Bsz, S, _ = x.shape
    slopes = alibi_slopes(DIFF_HEADS)
    c0 = CONV_WIDTH
    q0 = 3 * CONV_WIDTH
    g0 = q0 + 3 * DIFF_WIDTH
    for l in range(DEPTH):
        xn = rmsnorm(x, norm_mix_g[l])
        u = xn @ w_in[l]
        b_g, c_g, v_a = u[..., 0:c0], u[..., c0:2 * c0], u[..., 2 * c0:3 * c0]
        q = u[..., q0:q0 + DIFF_WIDTH].reshape(Bsz, S, DIFF_HEADS, 2, DIFF_HEAD_DIM)
        k = u[..., q0 + DIFF_WIDTH:q0 + 2 * DIFF_WIDTH].reshape(Bsz, S, DIFF_HEADS, 2, DIFF_HEAD_DIM)
        v = u[..., q0 + 2 * DIFF_WIDTH:g0].reshape(Bsz, S, DIFF_HEADS, DIFF_V_DIM)
        gates = jax.nn.sigmoid((u[..., g0:] + b_gate[l]).astype(jnp.float32)).astype(x.dtype)
        g_a, g_b = gates[..., :D_MODEL], gates[..., D_MODEL:]

        y_a = short_gated_conv(b_g, c_g, v_a, conv_w[l]) @ w_a_out[l]

        lam_init = lambda_init_fn(l)
        lam = (jnp.exp(jnp.sum(lambda_q1[l].astype(jnp.float32) * lambda_k1[l].astype(jnp.float32)))
               - jnp.exp(jnp.sum(lambda_q2[l].astype(jnp.float32) * lambda_k2[l].astype(jnp.float32)))
               + lam_init)
        o = diff_attention(q, k, v, lam, slopes)
        o = rmsnorm(o, subln_g[l]) * (1.0 - lam_init)
        y_b = o.reshape(Bsz, S, DIFF_WIDTH) @ w_b_out[l]

        x = x + (g_a * y_a + g_b * y_b) @ w_o[l]

        h = rmsnorm(x, norm_mlp_g[l]) @ w_mlp_in[l]
        x = x + jnp.square(jax.nn.relu(h)) @ w_mlp_out[l]
    return rmsnorm(x, norm_final_g)
```

```python
import math
import numpy as np
import ml_dtypes
import concourse.bass as bass
import concourse.mybir as mybir
from concourse.bass_utils import run_bass_kernel_spmd

F32 = mybir.dt.float32
BF16 = mybir.dt.bfloat16
AF = mybir.ActivationFunctionType
ALU = mybir.AluOpType
AX = mybir.AxisListType

ENGS = ("pe", "act", "dve", "pool", "sp")
N_DMA_SEMS = 12

D_MODEL = 1024
SEQ = 4096
BATCH = 4
BLK = 512
NEG = -30000.0
SLOPES = [2.0 ** (-8.0 * (h + 1) / 4) for h in range(4)]
LAM_INIT = 0.8 - 0.6 * math.exp(-0.3 * 0)
EPS = 1e-6
OWN_A = [0, 3, 4, 7]
OWN_B = [1, 2, 5, 6]
LSUM_LAG = 3


class Sched:
    def __init__(self, nc):
        self.nc = nc
        self.ops = {e: [] for e in ENGS}
        self.last_writer = {}
        self.readers = {}
        self.dma_count = {e: 0 for e in ENGS}
        self.last_real = {e: None for e in ENGS}

    def _add(self, eng, fn, reads, writes, is_dma):
        writers = set()
        for k in reads:
            assert k in self.last_writer, ("read of a buffer nobody has written yet", k)
        for k in list(reads) + list(writes):
            w = self.last_writer.get(k)
            if w is not None:
                writers.add(w)
        wars = set()
        for k in writes:
            for r in self.readers.get(k, ()):
                wars.add(r)
        deps = set()
        for d in writers:
            if (not is_dma) and d[0] == "c" and d[1] == eng and eng == "pe":
                continue
            deps.add(d)
        for d in wars:
            if (not is_dma) and d[0] == "c" and d[1] == eng:
                continue
            deps.add(d)
        idx = len(self.ops[eng])
        if is_dma:
            ev = ("d", eng, self.dma_count[eng])
            self.dma_count[eng] += 1
        else:
            ev = ("c", eng, idx)
            self.last_real[eng] = ev
        self.ops[eng].append(dict(fn=fn, deps=deps, ev=ev, is_dma=is_dma))
        for k in writes:
            self.last_writer[k] = ev
            self.readers[k] = []
        for k in reads:
            self.readers.setdefault(k, []).append(ev)
        return ev

    def op(self, eng, fn, reads=(), writes=()):
        return self._add(eng, fn, reads, writes, False)

    def dma(self, eng, fn, reads=(), writes=()):
        return self._add(eng, fn, reads, writes, True)

    def barrier(self):
        deps = set()
        for e in ENGS:
            if self.last_real[e] is not None:
                deps.add(self.last_real[e])
            n = self.dma_count[e]
            for j in range(max(0, n - N_DMA_SEMS), n):
                deps.add(("d", e, j))
        for e in ENGS:
            self.ops[e].append(dict(fn=None, deps=set(deps), ev=None, is_dma=False))

    def emit(self):
        nc = self.nc
        waited = {e: set() for e in ENGS}
        for e in ENGS:
            for o in self.ops[e]:
                for d in o["deps"]:
                    if d[0] == "c":
                        waited[d[1]].add(d[2])
        csem = {e: nc.alloc_semaphore("c_" + e) for e in ENGS}
        dsem = {e: [nc.alloc_semaphore("d_%s_%d" % (e, i)) for i in range(N_DMA_SEMS)]
                for e in ENGS if self.dma_count[e] > 0}
        cval = {e: {} for e in ENGS}
        for e in ENGS:
            c = 0
            for i, o in enumerate(self.ops[e]):
                if o["fn"] is not None and (not o["is_dma"]) and i in waited[e]:
                    c += 1
                    cval[e][i] = c
        self.n_waits = 0

        def sem_val(d):
            if d[0] == "c":
                return csem[d[1]], cval[d[1]][d[2]]
            return dsem[d[1]][d[2] % N_DMA_SEMS], 16 * (d[2] // N_DMA_SEMS + 1)

        def run(e, eng):
            seen = {}
            for i, o in enumerate(self.ops[e]):
                deps = set(o["deps"])
                if o["is_dma"]:
                    j = o["ev"][2]
                    if j >= N_DMA_SEMS:
                        deps.add(("d", e, j - N_DMA_SEMS))
                for d in sorted(deps):
                    s, v = sem_val(d)
                    key = id(s)
                    if seen.get(key, 0) >= v:
                        continue
                    seen[key] = v
                    eng.wait_ge(s, v)
                    self.n_waits += 1
                if o["fn"] is None:
                    continue
                ins = o["fn"](eng)
                if o["is_dma"]:
                    s, v = sem_val(o["ev"])
                    ins.then_inc(s, 16)
                elif i in cval[e]:
                    ins.then_inc(csem[e], 1)
            n = self.dma_count[e]
            for k in range(N_DMA_SEMS):
                cnt = (n - k + N_DMA_SEMS - 1) // N_DMA_SEMS if n > k else 0
                if cnt > 0:
                    eng.wait_ge(dsem[e][k], 16 * cnt)

        with nc.Block() as block:
            @block.tensor
            def _(eng):
                run("pe", eng)

            @block.scalar
            def _(eng):
                run("act", eng)

            @block.vector
            def _(eng):
                run("dve", eng)

            @block.gpsimd
            def _(eng):
                run("pool", eng)

            @block.sync
            def _(eng):
                run("sp", eng)


class Ring:
    def __init__(self, aps, name, keys=None):
        self.aps = aps
        self.keys = keys if keys is not None else ["%s%d" % (name, k) for k in range(len(aps))]
        self.i = 0

    def next(self):
        k = self.i % len(self.aps)
        self.i += 1
        return self.aps[k], self.keys[k]


class Arena:
    def __init__(self, big, base, limit):
        self.big = big
        self.p = base
        self.limit = limit

    def take(self, shape, dt):
        n = 1
        for s in shape[1:]:
            n *= s
        nbytes = n * (4 if dt == F32 else 2)
        nbytes = (nbytes + 31) // 32 * 32
        off = self.p
        self.p += nbytes
        assert self.p <= self.limit, ("SBUF arena overflow", self.p, self.limit)
        ap = self.big[:, off // 2: (off + nbytes) // 2]
        if dt == F32:
            ap = ap.bitcast(F32)
        ap = ap[:, 0:n]
        if len(shape) == 3:
            ap = ap.rearrange("p (a b) -> p a b", a=shape[1])
        elif len(shape) == 4:
            ap = ap.rearrange("p (a b c) -> p a b c", a=shape[1], b=shape[2])
        return ap


def build_program(debug=False):
    nc = bass.Bass("TRN2", target_bir_lowering=False)

    def din(name, shape, dt=F32):
        return nc.dram_tensor(name, shape, dt, kind="ExternalInput").ap()

    xs = din("xs", [SEQ, D_MODEL])
    xh = din("xh", [8, D_MODEL])
    w_in = din("w_in", [D_MODEL, 5120])
    w_a = din("w_a", [512, D_MODEL])
    w_b = din("w_b", [512, D_MODEL])
    w_o = din("w_o", [D_MODEL, D_MODEL])
    w_1 = din("w_1", [D_MODEL, 4096])
    w_2 = din("w_2", [4096, D_MODEL])
    gmix_d = din("gmix", [128, 8])
    gmlp_d = din("gmlp", [128, 8])
    bgate_d = din("bgate", [128, 16])
    convw_d = din("convw", [128, 12])
    gsub_d = din("gsub", [128, 1])
    lam_d = din("lam", [1, 256])
    gfin_d = din("gfin", [1, D_MODEL])
    btab_d = din("btab", [128, 512])
    ident_d = din("ident", [128, 128], BF16)
    tri_d = din("tri", [128, 128], BF16)
    out_d = nc.dram_tensor("out", [2048, D_MODEL], F32, kind="ExternalOutput").ap()
    dbg = {}
    if debug:
        dbg["KT"] = nc.dram_tensor("dbg_KT", [128, 4 * SEQ], BF16, kind="ExternalOutput").ap()
        dbg["V"] = nc.dram_tensor("dbg_V", [128, 32 * 512], BF16, kind="ExternalOutput").ap()
        dbg["QT"] = nc.dram_tensor("dbg_QT", [128, 4 * 2048], BF16, kind="ExternalOutput").ap()
        dbg["YA"] = nc.dram_tensor("dbg_YA", [128, 4 * 2048], BF16, kind="ExternalOutput").ap()
        dbg["ON"] = nc.dram_tensor("dbg_ON", [128, 4 * 2048], BF16, kind="ExternalOutput").ap()

    S = Sched(nc)
    TOT = 212000
    big = nc.alloc_sbuf_tensor("big", [128, TOT // 2], BF16)
    pers = Arena(big, 0, TOT)

    ident = pers.take([128, 128], BF16)
    tri = pers.take([128, 128], BF16)
    ones_bf = pers.take([128, 128], BF16)
    onesdiv = pers.take([128, 128], F32)
    btab = pers.take([128, 512], F32)
    gmix = pers.take([128, 8], F32)
    gmlp = pers.take([128, 8], F32)
    bgate = pers.take([128, 16], F32)
    convw = pers.take([128, 12], F32)
    gsub = pers.take([128, 8], F32)
    lamv = pers.take([128, 256], F32)
    lamw = pers.take([128, 128], F32)
    lams = pers.take([128, 8], F32)
    epsc = pers.take([128, 8], F32)
    onec = pers.take([128, 8], F32)
    nbgate = pers.take([128, 16], F32)
    gfin = pers.take([128, 1024], F32)
    hhalo = pers.take([128, 4, 8], F32)
    stats = [pers.take([128, 8], F32) for _ in range(6)]
    stat_ring = Ring(stats, "stat")
    YA_OFF = pers.p
    ya_pre = pers.take([128, 4, 2048], BF16)
    ON_OFF = pers.p
    O_n = pers.take([128, 4, 2048], BF16)
    P3_BASE = pers.p
    QT = pers.take([128, 4, 2048], BF16)
    KT = pers.take([128, 4, SEQ], BF16)
    Vt = pers.take([128, 32, 512], BF16)
    SCR_BASE = pers.p
    TOP = TOT - 32768
    KEEP_OFF = TOP - 8192
    keepA = Arena(big, KEEP_OFF, TOP)
    xnT_keep = keepA.take([128, 8, 512], BF16)
    top3 = Arena(big, TOP, TOT)
    Wa = top3.take([128, 4, 1024], BF16)
    Wb = top3.take([128, 4, 1024], BF16)
    Wo = top3.take([128, 8, 1024], BF16)

    ps_all = nc.alloc_psum_tensor("ps_all", [128, 8 * 512], F32)

    def bank(b, n=1):
        return ps_all[:, b * 512:(b + n) * 512]

    def bank_bf_T(b):
        return bank(b).bitcast(BF16).rearrange("p (c t) -> p c t", c=8)

    def ld(eng, dst, src, key):
        S.dma(eng, lambda e: e.dma_start(out=dst, in_=src), writes=[key])

    ld("sp", ident, ident_d, "ident")
    ld("sp", tri, tri_d, "tri")
    ld("sp", btab, btab_d, "btab")
    ld("sp", gmix, gmix_d, "gmix")
    ld("sp", gmlp, gmlp_d, "gmlp")
    ld("sp", bgate, bgate_d, "bgate")
    ld("sp", convw, convw_d, "convw")
    ld("sp", gsub[:, 0:1], gsub_d, "gsub0")
    ld("sp", lamv, lam_d.partition_broadcast(128), "lamv")
    ld("sp", gfin, gfin_d.partition_broadcast(128), "gfin")
    S.op("pool", lambda e: e.memset(ones_bf, 1.0), writes=["ones_bf"])
    S.op("pool", lambda e: e.memset(onesdiv, 1.0 / 128.0), writes=["onesdiv"])
    S.op("pool", lambda e: e.memset(epsc, EPS), writes=["epsc"])
    S.op("pool", lambda e: e.memset(onec, 1.0), writes=["onec"])
    S.op("dve", lambda e: e.tensor_scalar_mul(nbgate, bgate, -1.0), reads=["bgate"], writes=["nbgate"])
    S.op("dve", lambda e: e.tensor_tensor(lamw[:, 0:64], lamv[:, 0:64], lamv[:, 64:128], ALU.mult),
         reads=["lamv"], writes=["lamw"])
    S.op("dve", lambda e: e.tensor_tensor(lamw[:, 64:128], lamv[:, 128:192], lamv[:, 192:256], ALU.mult),
         reads=["lamv"], writes=["lamw"])
    S.op("dve", lambda e: e.reduce_sum(lams[:, 0:1], lamw[:, 0:64], axis=AX.X), reads=["lamw"], writes=["lams"])
    S.op("dve", lambda e: e.reduce_sum(lams[:, 1:2], lamw[:, 64:128], axis=AX.X), reads=["lamw"], writes=["lams"])
    S.op("act", lambda e: e.activation(lams[:, 2:4], lams[:, 0:2], AF.Exp), reads=["lams"], writes=["lams"])
    S.op("dve", lambda e: e.tensor_tensor(lams[:, 4:5], lams[:, 3:4], lams[:, 2:3], ALU.subtract),
         reads=["lams"], writes=["lams"])
    S.op("dve", lambda e: e.tensor_scalar_add(lams[:, 5:6], lams[:, 4:5], -LAM_INIT), reads=["lams"], writes=["lams"])
    neg_lam = lams[:, 5:6]
    S.op("dve", lambda e: e.tensor_scalar_mul(gsub[:, 1:2], gsub[:, 0:1], 1.0 - LAM_INIT), reads=["gsub0"], writes=["gsub"])

    def wload(dst, src, key):
        S.dma("pool", lambda e: e.dma_start(out=dst, in_=src), writes=[key])

    def w_cols(w, c0, n):
        return w[:, c0:c0 + n].rearrange("(c p) n -> p c n", p=128)

    def norm_T(xt, xkey, ntok, gcols, gkey, dst, dkey, xb_ring, pst_ring):
        st, skey = stat_ring.next()
        xb, xbkey = xb_ring.next()
        S.op("dve", lambda e: e.memset(st[:ntok, 0:1], 0.0), writes=[skey])
        S.op("act", lambda e: e.activation(xb[:ntok], xt[:ntok], AF.Square, accum_out=st[:ntok, 0:1]),
             reads=[xkey, skey], writes=[xbkey, skey])
        S.op("act", lambda e: e.activation(st[:ntok, 1:2], st[:ntok, 0:1], AF.Ln, bias=epsc[:ntok, 0:1],
                                           scale=1.0 / D_MODEL), reads=[skey, "epsc"], writes=[skey])
        S.op("act", lambda e: e.activation(st[:ntok, 2:3], st[:ntok, 1:2], AF.Exp, scale=-0.5),
             reads=[skey], writes=[skey])
        S.op("dve", lambda e: e.tensor_scalar_mul(xb[:ntok], xt[:ntok], st[:ntok, 2:3]),
             reads=[xkey, skey], writes=[xbkey])
        pst, pkey = pst_ring.next()
        for c in range(8):
            S.op("pe", lambda e, c=c: e.transpose(pst[:, c, :ntok], xb[:ntok, c * 128:(c + 1) * 128],
                                                   ident[:ntok, :ntok]),
                 reads=[xbkey, "ident"], writes=[pkey])
        S.op("dve", lambda e: e.tensor_tensor(dst, pst[:, :, :ntok],
                                              gcols.unsqueeze(2).to_broadcast([128, 8, ntok]), ALU.mult),
             reads=[pkey, gkey], writes=[dkey])
        return st, skey

    def mm(out, lhsT, rhs, start, stop, reads, writes, tp=None):
        if tp is None:
            S.op("pe", lambda e: e.matmul(out, lhsT, rhs, start=start, stop=stop), reads=reads, writes=writes)
        else:
            S.op("pe", lambda e: e.matmul(out, lhsT, rhs, start=start, stop=stop, tile_position=tp),
                 reads=reads, writes=writes)

    def dump(name, src, key):
        if debug:
            S.dma("sp", lambda e: e.dma_start(out=dbg[name], in_=src), reads=[key])

    def npre_dma(src_rows, ntok, xt_ring):
        xt, xkey = xt_ring.next()
        S.dma("sp", lambda e: e.dma_start(out=xt[:ntok], in_=src_rows), writes=[xkey])
        return xt, xkey

    def npre_act(xt, xkey, ntok, xb_ring):
        st, skey = stat_ring.next()
        xb, xbkey = xb_ring.next()
        S.op("dve", lambda e: e.memset(st[:ntok, 0:1], 0.0), writes=[skey])
        S.op("act", lambda e: e.activation(xb[:ntok], xt[:ntok], AF.Square, accum_out=st[:ntok, 0:1]),
             reads=[xkey, skey], writes=[xbkey, skey])
        S.op("act", lambda e: e.activation(st[:ntok, 1:2], st[:ntok, 0:1], AF.Ln, bias=epsc[:ntok, 0:1],
                                           scale=1.0 / D_MODEL), reads=[skey, "epsc"], writes=[skey])
        S.op("act", lambda e: e.activation(st[:ntok, 2:3], st[:ntok, 1:2], AF.Exp, scale=-0.5),
             reads=[skey], writes=[skey])
        return xt, xkey, xb, xbkey, st, skey

    def npre_dve(info, ntok):
        xt, xkey, xb, xbkey, st, skey = info
        S.op("dve", lambda e: e.tensor_scalar_mul(xb[:ntok], xt[:ntok], st[:ntok, 2:3]),
             reads=[xkey, skey], writes=[xbkey])
        return xb, xbkey

    def norm_pre(src_rows, ntok, xt_ring, xb_ring):
        xt, xkey = npre_dma(src_rows, ntok, xt_ring)
        return npre_dve(npre_act(xt, xkey, ntok, xb_ring), ntok)

    def norm_post(xb, xbkey, ntok, gcols, gkey, dst, dkey, pst_ring):
        pst, pkey = pst_ring.next()
        for c in range(8):
            S.op("pe", lambda e, c=c: e.transpose(pst[:, c, :ntok], xb[:ntok, c * 128:(c + 1) * 128],
                                                   ident[:ntok, :ntok]),
                 reads=[xbkey, "ident"], writes=[pkey])
        S.op("dve", lambda e: e.tensor_tensor(dst, pst[:, :, :ntok],
                                              gcols.unsqueeze(2).to_broadcast([128, 8, ntok]), ALU.mult),
             reads=[pkey, gkey], writes=[dkey])

    evac_flip = [0]

    def evac(dst, src, reads, writes):
        evac_flip[0] ^= 1
        if evac_flip[0]:
            S.op("act", lambda e: e.copy(dst, src), reads=reads, writes=writes)
        else:
            S.op("dve", lambda e: e.tensor_copy(dst, src), reads=reads, writes=writes)

    arO = Arena(big, 0, TOT)
    arO.p = ON_OFF
    xb_ring = Ring([arO.take([128, 1024], BF16) for _ in range(4)], "xb")
    cg_ring = Ring([arO.take([128, 512], F32) for _ in range(1)], "cg")
    hb_ring = Ring([arO.take([128, 516], F32) for _ in range(1)], "hb")
    acc_ring = Ring([arO.take([128, 512], F32) for _ in range(1)], "cacc")
    assert arO.p <= ON_OFF + 16384
    ar = Arena(big, SCR_BASE, KEEP_OFF)
    Wc = ar.take([128, 8, 1536], BF16)
    Wk = ar.take([128, 8, 512], BF16)
    Wv = ar.take([128, 8, 512], BF16)
    xt_a = ar.take([128, 1024], F32)
    xnTh = ar.take([128, 8, 8], BF16)
    ar2 = Arena(big, TOP, TOT)
    Wq = ar2.take([128, 8, 512], BF16)
    xnT_bufs = [ar2.take([128, 8, 512], BF16) for _ in range(2)]
    xt_ring = Ring([xt_a] + [ar2.take([128, 1024], F32) for _ in range(2)], "xt")
    pst_ring = Ring([bank_bf_T(0), bank_bf_T(1)], "pst", keys=["pb0", "pb1"])
    mm_ring = Ring([bank(b) for b in range(2, 8)], "pmm", keys=["pb%d" % b for b in range(2, 8)])

    wload(Wv, w_cols(w_in, 2560, 512), "Wv")
    wload(Wk, w_cols(w_in, 2048, 512), "Wk")

    def wload_after(dst, src, key, after):
        S.dma("pool", lambda e: e.dma_start(out=dst, in_=src), reads=after, writes=[key])

    order = [4, 5, 6, 7, 0, 1, 2, 3]

    def xk_of(blk):
        return ["xn%d_%d" % (blk, t) for t in range(4)]

    def kpart(blk, xnT, hs):
        def f():
            for h in hs:
                pb, pkey = mm_ring.next()
                for c in range(8):
                    mm(pb, Wk[:, c, h * 128:(h + 1) * 128], xnT[:, c, :], c == 0, c == 7,
                       reads=["Wk"] + xk_of(blk), writes=[pkey])
                S.op("act", lambda e, pb=pb, h=h: e.copy(KT[:, h, blk * 512:(blk + 1) * 512], pb),
                     reads=[pkey], writes=["KT%d" % blk])
        return f

    def vpart(blk, xnT, ts):
        def f():
            for t in ts:
                pb, pkey = mm_ring.next()
                for c in range(8):
                    mm(pb, xnT[:, c, t * 128:(t + 1) * 128], Wv[:, c, :], c == 0, c == 7,
                       reads=["Wv", xk_of(blk)[t]], writes=[pkey])
                evac(Vt[:, blk * 4 + t, :], pb, [pkey], ["V%d" % (blk * 4 + t)])
        return f

    def qpart(i, xnT):
        cols = slice(i * 512, (i + 1) * 512)

        def f():
            for h in range(4):
                pb, pkey = mm_ring.next()
                for c in range(8):
                    mm(pb, Wq[:, c, h * 128:(h + 1) * 128], xnT[:, c, :], c == 0, c == 7,
                       reads=["Wq"] + xk_of(i), writes=[pkey])
                evac(QT[:, h, cols], pb, [pkey], ["QT%d" % i])
        return f

    def cpart(i, xnT, ch):
        cols = slice(i * 512, (i + 1) * 512)
        xkeys = xk_of(i)

        def f():
            pc, pckey = mm_ring.next()
            for c in range(8):
                mm(pc, Wc[:, c, 512 + ch * 128:512 + (ch + 1) * 128], xnT[:, c, :], c == 0, c == 7,
                   reads=["Wc_cv"] + xkeys, writes=[pckey])
            pv, pvkey = mm_ring.next()
            for c in range(8):
                mm(pv, Wc[:, c, 1024 + ch * 128:1024 + (ch + 1) * 128], xnT[:, c, :], c == 0, c == 7,
                   reads=["Wc_cv"] + xkeys, writes=[pvkey])
            pg, pgkey = mm_ring.next()
            for c in range(8):
                mm(pg, Wc[:, c, ch * 128:(ch + 1) * 128], xnT[:, c, :], c == 0, c == 7,
                   reads=["Wc_b"] + xkeys, writes=[pgkey])
            cg, cgkey = cg_ring.next()
            hb, hbkey = hb_ring.next()
            acc, acckey = acc_ring.next()
            S.op("act", lambda e: e.copy(cg, pc), reads=[pckey], writes=[cgkey])
            S.op("dve", lambda e: e.tensor_copy(hb[:, 0:2], hhalo[:, ch, 2 * i:2 * i + 2]),
                 reads=["hhalo"], writes=[hbkey])
            S.op("dve", lambda e: e.tensor_tensor(hb[:, 2:514], pv, cg, ALU.mult),
                 reads=[pvkey, cgkey], writes=[hbkey])
            S.op("dve", lambda e: e.tensor_scalar_mul(acc, hb[:, 2:514], convw[:, ch * 3 + 2:ch * 3 + 3]),
                 reads=[hbkey, "convw"], writes=[acckey])
            S.op("dve", lambda e: e.scalar_tensor_tensor(
                acc, hb[:, 1:513], convw[:, ch * 3 + 1:ch * 3 + 2], acc, ALU.mult, ALU.add),
                 reads=[hbkey, "convw", acckey], writes=[acckey])
            S.op("dve", lambda e: e.scalar_tensor_tensor(
                acc, hb[:, 0:512], convw[:, ch * 3:ch * 3 + 1], acc, ALU.mult, ALU.add),
                 reads=[hbkey, "convw", acckey], writes=[acckey])
            S.op("dve", lambda e: e.tensor_tensor(ya_pre[:, ch, cols], pg, acc, ALU.mult),
                 reads=[pgkey, acckey], writes=["ya_pre%d" % i])
        return f

    def halo_part():
        xb, xbkey = norm_pre(xh, 8, xt_ring, xb_ring)
        norm_post(xb, xbkey, 8, gmix, "gmix", xnTh, "xnTh", pst_ring)
        for ch in range(4):
            pc, pckey = mm_ring.next()
            for c in range(8):
                mm(pc[:, 0:8], Wc[:, c, 512 + ch * 128:512 + (ch + 1) * 128], xnTh[:, c, :], c == 0, c == 7,
                   reads=["Wc_cv", "xnTh"], writes=[pckey])
            pv, pvkey = mm_ring.next()
            for c in range(8):
                mm(pv[:, 0:8], Wc[:, c, 1024 + ch * 128:1024 + (ch + 1) * 128], xnTh[:, c, :], c == 0, c == 7,
                   reads=["Wc_cv", "xnTh"], writes=[pvkey])
            cg, cgkey = cg_ring.next()
            S.op("act", lambda e, cg=cg, pc=pc: e.copy(cg[:, 0:8], pc[:, 0:8]), reads=[pckey], writes=[cgkey])
            S.op("dve", lambda e, cg=cg, pv=pv, ch=ch: e.tensor_tensor(hhalo[:, ch, :], pv[:, 0:8], cg[:, 0:8], ALU.mult),
                 reads=[pvkey, cgkey], writes=["hhalo"])

    flat = []
    blk_last_part = {}
    def xnT_of(si):
        return xnT_keep if order[si] == 0 else xnT_bufs[si % 2]

    for si, blk in enumerate(order):
        xnT = xnT_of(si)
        parts = [vpart(blk, xnT, [0, 1]), vpart(blk, xnT, [2, 3]), kpart(blk, xnT, [0, 1]), kpart(blk, xnT, [2, 3])]
        if blk < 4:
            parts += [qpart(blk, xnT)] + [cpart(blk, xnT, ch) for ch in range(4)]
        if blk == 0:
            flat.append(halo_part)
        for p_ in parts:
            flat.append(p_)
        blk_last_part[si] = len(flat) - 1

    ev_at = {}
    for si in range(1, 8):
        for t in range(4):
            pp = blk_last_part[si - 1] - 3 + t
            for name, off in (("post", 0), ("dve", -2), ("act", -3), ("dma", -4)):
                ev_at.setdefault(max(pp + off, -1), []).append((name, si, t))
    st1 = {}

    def do_ev(name, si, t):
        blk = order[si]
        if name == "dma":
            r0 = blk * 512 + t * 128
            st1[(si, t)] = npre_dma(xs[r0:r0 + 128, :], 128, xt_ring)
        elif name == "act":
            xt, xkey = st1[(si, t)]
            st1[(si, t)] = npre_act(xt, xkey, 128, xb_ring)
        elif name == "dve":
            st1[(si, t)] = npre_dve(st1[(si, t)], 128)
        else:
            xb, xbkey = st1[(si, t)]
            norm_post(xb, xbkey, 128, gmix, "gmix", xnT_of(si)[:, :, t * 128:(t + 1) * 128],
                      "xn%d_%d" % (blk, t), pst_ring)

    def run_evs(j):
        evs = ev_at.get(j, [])
        for nm in ("post", "dve", "act", "dma"):
            for (name, si, t) in evs:
                if name == nm:
                    do_ev(name, si, t)

    do_ev("dma", 0, 0)
    do_ev("dma", 0, 1)
    do_ev("dma", 0, 2)
    do_ev("act", 0, 0)
    do_ev("act", 0, 1)
    do_ev("dve", 0, 0)
    do_ev("dma", 0, 3)
    do_ev("act", 0, 2)
    do_ev("dve", 0, 1)
    do_ev("post", 0, 0)
    do_ev("act", 0, 3)
    do_ev("dve", 0, 2)
    do_ev("post", 0, 1)
    do_ev("dve", 0, 3)
    do_ev("post", 0, 2)
    do_ev("post", 0, 3)
    early = ev_at.get(-1, [])
    for (si_, t_) in sorted(set((si, t) for (_, si, t) in early)):
        for nm in ("dma", "act", "dve", "post"):
            if (nm, si_, t_) in early:
                do_ev(nm, si_, t_)
    for j, fn in enumerate(flat):
        fn()
        run_evs(j)
        if j == 0:
            wload_after(Wc[:, :, 512:1536], w_cols(w_in, 512, 1024), "Wc_cv", ["xn5_0"])
            wload_after(Wq, w_cols(w_in, 1536, 512), "Wq", ["xn5_0"])
            wload_after(Wc[:, :, 0:512], w_cols(w_in, 0, 512), "Wc_b", ["xn5_0"])
    dump("KT", KT.rearrange("p a b -> p (a b)"), "KT3")
    dump("V", Vt.rearrange("p a b -> p (a b)"), "V15")
    dump("QT", QT.rearrange("p a b -> p (a b)"), "QT3")
    dump("YA", ya_pre.rearrange("p a b -> p (a b)"), "ya_pre3")
    S.barrier()

    wload(Wa, w_a.rearrange("(c p) n -> p c n", p=128), "Wa")
    wload(Wb, w_b.rearrange("(c p) n -> p c n", p=128), "Wb")
    wload(Wo, w_o.rearrange("(c p) n -> p c n", p=128), "Wo")
    ar = Arena(big, SCR_BASE, TOP)
    PT_ring = Ring([ar.take([128, 2, 512], BF16) for _ in range(8)], "PT")
    ectx = []
    for k in range(2):
        ectx.append(dict(Ocp=ar.take([128, 2, 512], F32), Lcp=ar.take([128, 2, 512], F32),
                         o=ar.take([128, 512], F32), sq=ar.take([128, 512], F32), ln=ar.take([128, 512], F32),
                         k="e%d" % k))
    Sps = ps_all[:, 0:2048].rearrange("p (b m n) -> p b m n", b=2, m=2)
    Ops = ps_all[:, 2048:3072].rearrange("p (m n) -> p m n", m=2)
    Lps = ps_all[:, 3072:4096].rearrange("p (m n) -> p m n", m=2)

    steps = []
    hid = 0
    for i in range(4):
        for h in range(4):
            plist = [i] + list(range(i)) + [4 + x for x in range(i + 1)]
            tiles = [(p, j) for p in plist for j in range(4)]
            for n, (p, j) in enumerate(tiles):
                steps.append(dict(i=i, h=h, p=p, j=j, cs=(128 * j if p == i else 0), diag=(p == i),
                                  first=(n == 0), last=(n == len(tiles) - 1), hid=hid))
            hid += 1

    s_cnt = [0]
    s_of = {}

    def take_s():
        k = s_cnt[0] % 2
        s_cnt[0] += 1
        return k

    def emit_qk(n):
        st = steps[n]
        i, h, p, j, cs = st["i"], st["h"], st["p"], st["j"], st["cs"]
        buf = take_s()
        s_of[n] = buf
        skey = "S%d" % buf
        k0 = p * 512 + j * 128
        q0 = i * 512
        if st["diag"]:
            for m in range(2):
                mm(Sps[:, buf, m, cs:cs + 128], KT[64 * m:64 * m + 64, h, k0:k0 + 128],
                   QT[64 * m:64 * m + 64, h, q0 + cs:q0 + cs + 128], True, False,
                   reads=["KT%d" % p, "QT%d" % i], writes=[skey], tp=(64 * m, 0))
            for m in range(2):
                mm(Sps[:, buf, m, cs:cs + 128], ident, tri, False, True, reads=["ident", "tri"], writes=[skey])
            if cs + 128 < 512:
                for m in range(2):
                    mm(Sps[:, buf, m, cs + 128:512], KT[64 * m:64 * m + 64, h, k0:k0 + 128],
                       QT[64 * m:64 * m + 64, h, q0 + cs + 128:q0 + 512], True, True,
                       reads=["KT%d" % p, "QT%d" % i], writes=[skey], tp=(64 * m, 0))
        else:
            for m in range(2):
                mm(Sps[:, buf, m, :], KT[64 * m:64 * m + 64, h, k0:k0 + 128],
                   QT[64 * m:64 * m + 64, h, q0:q0 + 512], True, True,
                   reads=["KT%d" % p, "QT%d" % i], writes=[skey], tp=(64 * m, 0))

    pt_of = {}
    pt_user = {}
    pv_done = set()
    ls_done = set()

    def emit_exp(n):
        st = steps[n]
        cs = st["cs"]
        buf = s_of[n]
        PT, ptkey = PT_ring.next()
        prev = pt_user.get(ptkey)
        assert prev is None or (prev in pv_done and prev in ls_done), ("PT ring too small", n, prev)
        pt_user[ptkey] = n
        pt_of[n] = (PT, ptkey)
        bidx = ((st["i"] * 8 + st["p"]) * 4 + st["j"]) * 4 + st["h"]
        S.op("act", lambda e: e.activation(PT[:, :, cs:512], Sps[:, buf, :, cs:512], AF.Exp,
                                           bias=btab[:, bidx:bidx + 1], scale=0.125),
             reads=["S%d" % buf, "btab"], writes=[ptkey])

    def emit_pv(n):
        st = steps[n]
        cs, h = st["cs"], st["h"]
        PT, ptkey = pt_of[n]
        kt = st["p"] * 4 + st["j"]
        pv_done.add(n)
        for m in range(2):
            mm(Ops[:, m, cs:512], Vt[:, kt, h * 128:(h + 1) * 128], PT[:, m, cs:512], st["first"], st["last"],
               reads=["V%d" % kt, ptkey], writes=["O"])

    def emit_lsum(n):
        st = steps[n]
        cs = st["cs"]
        PT, ptkey = pt_of[n]
        ls_done.add(n)
        for m in range(2):
            mm(Lps[:, m, cs:512], ones_bf, PT[:, m, cs:512], st["first"], st["last"],
               reads=["ones_bf", ptkey], writes=["L"])

    def emit_ocopy(n):
        c = ectx[steps[n]["hid"] % 2]
        S.op("dve", lambda e: e.tensor_copy(c["Ocp"], Ops), reads=["O"], writes=[c["k"] + "O"])

    def emit_chain(n):
        c = ectx[steps[n]["hid"] % 2]
        k = c["k"]
        S.op("dve", lambda e: e.tensor_copy(c["Lcp"], Lps), reads=["L"], writes=[k + "L"])
        S.op("dve", lambda e: e.reciprocal(c["Lcp"], c["Lcp"]), reads=[k + "L"], writes=[k + "L"])
        S.op("dve", lambda e: e.tensor_tensor(c["Ocp"], c["Ocp"], c["Lcp"], ALU.mult),
             reads=[k + "O", k + "L"], writes=[k + "O"])
        S.op("dve", lambda e: e.scalar_tensor_tensor(c["o"], c["Ocp"][:, 1, :], neg_lam, c["Ocp"][:, 0, :],
                                                     ALU.mult, ALU.add),
             reads=[k + "O", "lams"], writes=[k + "o"])
        S.op("dve", lambda e: e.tensor_tensor(c["sq"], c["o"], c["o"], ALU.mult), reads=[k + "o"], writes=[k + "sq"])

    def emit_ms(n):
        st = steps[n]
        c = ectx[st["hid"] % 2]
        k = c["k"]
        i, h = st["i"], st["h"]
        cols = slice(i * 512, (i + 1) * 512)
        buf = take_s()
        mm(Sps[:, buf, 0, :], onesdiv, c["sq"], True, True, reads=["onesdiv", k + "sq"], writes=["S%d" % buf])
        S.op("act", lambda e: e.activation(c["ln"], Sps[:, buf, 0, :], AF.Ln, bias=epsc[:, 0:1], scale=1.0),
             reads=["S%d" % buf, "epsc"], writes=[k + "ln"])
        S.op("act", lambda e: e.activation(c["ln"], c["ln"], AF.Exp, scale=-0.5), reads=[k + "ln"], writes=[k + "ln"])
        S.op("dve", lambda e: e.scalar_tensor_tensor(O_n[:, h, cols], c["o"], gsub[:, 1:2], c["ln"], ALU.mult, ALU.mult),
             reads=[k + "o", "gsub", k + "ln"], writes=["O_n%d" % i])

    NS = len(steps)
    emit_qk(0)
    lq = []
    hold_until = 0
    ms_due = []
    for n in range(NS + 1):
        if n < NS:
            emit_exp(n)
        while ms_due and ms_due[0][0] <= n:
            emit_ms(ms_due.pop(0)[1])
        if n + 1 < NS:
            emit_qk(n + 1)
        if n >= 1:
            emit_pv(n - 1)
            if steps[n - 1]["last"]:
                emit_ocopy(n - 1)
        if n < NS:
            lq.append(n)
        cnt = 0
        while lq and lq[0] <= n - LSUM_LAG and n >= hold_until and cnt < 2:
            q = lq.pop(0)
            emit_lsum(q)
            cnt += 1
            if steps[q]["last"]:
                emit_chain(q)
                hold_until = n + 3
                ms_due.append((n + 11, q))
                break
    while lq:
        q = lq.pop(0)
        emit_lsum(q)
        if steps[q]["last"]:
            emit_chain(q)
            ms_due.append((0, q))
    for _, q in ms_due:
        emit_ms(q)
    dump("ON", O_n.rearrange("p a b -> p (a b)"), "O_n3")
    S.barrier()

    ar = Arena(big, P3_BASE, KEEP_OFF)
    x_mid = ar.take([128, 16, 1024], F32)
    B_BASE = ar.p
    xnT3 = [xnT_keep, ar.take([128, 8, 512], BF16)]
    mergedT = ar.take([128, 8, 512], BF16)
    Wg_ring = Ring([ar.take([128, 8, 256], BF16) for _ in range(4)], "Wg")
    tf_ring = Ring([ar.take([128, 512], F32) for _ in range(6)], "tf")
    xb1_ring = Ring([ar.take([128, 1024], BF16) for _ in range(4)], "xb1_")
    xb2_ring = Ring([ar.take([128, 1024], BF16) for _ in range(4)], "xb2_")
    assert ON_OFF == YA_OFF + 16384
    xn2T_all = big[:, YA_OFF // 2: YA_OFF // 2 + 16384].rearrange("p (c t) -> p c t", c=8)
    pst_ring = Ring([bank_bf_T(7)], "pst", keys=["pb7"])
    mm_ring = Ring([bank(b) for b in range(0, 4)], "pmm", keys=["pb%d" % b for b in range(0, 4)])
    z_ring = Ring([bank(b) for b in (4, 5, 6)], "pz", keys=["pb4", "pb5", "pb6"])

    def load_x(T, after=()):
        S.dma("sp", lambda e: e.dma_start(out=x_mid[:, T, :], in_=xs[T * 128:(T + 1) * 128, :]),
              reads=list(after), writes=["xm%d" % T])

    def nsb_act(xt, xkey, xb_ring):
        st, skey = stat_ring.next()
        xb, xbkey = xb_ring.next()
        S.op("dve", lambda e: e.memset(st[:, 0:1], 0.0), writes=[skey])
        S.op("dve", lambda e: e.scalar_tensor_tensor(xb, xt, 1.0, xt, ALU.mult, ALU.mult, accum_out=st[:, 0:1]),
             reads=[xkey, skey], writes=[xbkey, skey])
        S.op("act", lambda e: e.activation(st[:, 1:2], st[:, 0:1], AF.Ln, bias=epsc[:, 0:1],
                                           scale=1.0 / D_MODEL), reads=[skey, "epsc"], writes=[skey])
        S.op("act", lambda e: e.activation(st[:, 2:3], st[:, 1:2], AF.Exp, scale=-0.5),
             reads=[skey], writes=[skey])
        return xt, xkey, xb, xbkey, st, skey

    def nsb_dve(info):
        xt, xkey, xb, xbkey, st, skey = info
        S.op("dve", lambda e: e.tensor_scalar_mul(xb, xt, st[:, 2:3]), reads=[xkey, skey], writes=[xbkey])
        return xb, xbkey

    def norm_post2(xbinfo, gcols, gkey, dst, dkeys):
        xb, xbkey = xbinfo
        pst, pkey = pst_ring.next()
        for c in range(8):
            S.op("pe", lambda e, c=c: e.transpose(pst[:, c, :], xb[:, c * 128:(c + 1) * 128], ident),
                 reads=[xbkey, "ident"], writes=[pkey])
        S.op("dve", lambda e: e.tensor_tensor(dst, pst, gcols.unsqueeze(2).to_broadcast([128, 8, 128]), ALU.mult),
             reads=[pkey, gkey], writes=dkeys)

    def gates_step(i, k):
        cols = slice(i * 512, (i + 1) * 512)
        xn = xnT3[i % 2]
        xk3 = ["x3_%d_%d" % (i, t) for t in range(4)] if i > 0 else ["xn0_%d" % t for t in range(4)]

        def f():
            ga_t, gakey = Wg_ring.next()
            wload(ga_t, w_cols(w_in, 3072 + 256 * k, 256), gakey)
            gb_t, gbkey = Wg_ring.next()
            wload(gb_t, w_cols(w_in, 4096 + 256 * k, 256), gbkey)
            for cc in range(2):
                dm = 2 * k + cc
                pga, pgak = mm_ring.next()
                for c in range(8):
                    mm(pga, ga_t[:, c, cc * 128:(cc + 1) * 128], xn[:, c, :], c == 0, c == 7,
                       reads=[gakey] + xk3, writes=[pgak])
                pgb, pgbk = mm_ring.next()
                for c in range(8):
                    mm(pgb, gb_t[:, c, cc * 128:(cc + 1) * 128], xn[:, c, :], c == 0, c == 7,
                       reads=[gbkey] + xk3, writes=[pgbk])
                t1, t1k = tf_ring.next()
                t2, t2k = tf_ring.next()
                S.op("act", lambda e, t1=t1, pga=pga, dm=dm: e.activation(t1, pga, AF.Exp, bias=nbgate[:, dm:dm + 1],
                                                                        scale=-1.0),
                     reads=[pgak, "nbgate"], writes=[t1k])
                S.op("act", lambda e, t2=t2, pgb=pgb, dm=dm: e.activation(t2, pgb, AF.Exp, bias=nbgate[:, 8 + dm:9 + dm],
                                                                        scale=-1.0),
                     reads=[pgbk, "nbgate"], writes=[t2k])
                S.op("act", lambda e, t1=t1: e.activation(t1, t1, AF.Ln, bias=onec[:, 0:1], scale=1.0),
                     reads=[t1k, "onec"], writes=[t1k])
                S.op("act", lambda e, t2=t2: e.activation(t2, t2, AF.Ln, bias=onec[:, 0:1], scale=1.0),
                     reads=[t2k, "onec"], writes=[t2k])
                S.op("act", lambda e, t1=t1: e.activation(t1, t1, AF.Exp, scale=-1.0), reads=[t1k], writes=[t1k])
                S.op("act", lambda e, t2=t2: e.activation(t2, t2, AF.Exp, scale=-1.0), reads=[t2k], writes=[t2k])
                pya, pyak = mm_ring.next()
                for c in range(4):
                    mm(pya, Wa[:, c, dm * 128:(dm + 1) * 128], ya_pre[:, c, cols], c == 0, c == 3,
                       reads=["Wa", "ya_pre%d" % i], writes=[pyak])
                S.op("dve", lambda e, t1=t1, pya=pya: e.tensor_tensor(t1, pya, t1, ALU.mult),
                     reads=[pyak, t1k], writes=[t1k])
                pyb, pybk = mm_ring.next()
                for c in range(4):
                    mm(pyb, Wb[:, c, dm * 128:(dm + 1) * 128], O_n[:, c, cols], c == 0, c == 3,
                       reads=["Wb", "O_n%d" % i], writes=[pybk])
                S.op("dve", lambda e, t2=t2, pyb=pyb: e.tensor_tensor(t2, pyb, t2, ALU.mult),
                     reads=[pybk, t2k], writes=[t2k])
                S.op("dve", lambda e, t1=t1, t2=t2, dm=dm: e.tensor_tensor(mergedT[:, dm, :], t1, t2, ALU.add),
                     reads=[t1k, t2k], writes=["mT%d" % dm])
        return f

    def z_tile(i, t):
        T = 4 * i + t
        mkeys = ["mT%d" % dm for dm in range(8)]

        def f():
            for half in range(2):
                pz, pzk = z_ring.next()
                for m in range(8):
                    mm(pz, mergedT[:, m, t * 128:(t + 1) * 128], Wo[:, m, half * 512:(half + 1) * 512],
                       m == 0, m == 7, reads=["Wo"] + mkeys, writes=[pzk])
                S.op("dve", lambda e, half=half, pz=pz: e.tensor_tensor(
                    x_mid[:, T, half * 512:(half + 1) * 512], pz, x_mid[:, T, half * 512:(half + 1) * 512], ALU.add),
                     reads=[pzk, "xm%d" % T], writes=["xm%d" % T])
        return f

    flatA = []
    for i in range(4):
        for k in range(4):
            flatA.append(gates_step(i, k))
        for t in range(4):
            flatA.append(z_tile(i, t))
    evA = {}

    def addA(pos, ev):
        evA.setdefault(pos, []).append(ev)

    for i in range(4):
        for t in range(4):
            if i >= 1:
                base = 8 * (i - 1)
                addA(base + t - 1, ("act1", i, t))
                addA(base + t, ("dve1", i, t))
                addA(base + t + 2, ("post1", i, t))
            base = 8 * i + 4 + t
            addA(base, ("act2", i, t))
            addA(base + 1, ("dve2", i, t))
            addA(base + 3, ("post2", i, t))
    stA = {}

    def do_evA(name, i, t):
        T = 4 * i + t
        if name == "ldx":
            load_x(T)
        elif name == "act1":
            stA[("n1", i, t)] = nsb_act(x_mid[:, T, :], "xm%d" % T, xb1_ring)
        elif name == "dve1":
            stA[("n1", i, t)] = nsb_dve(stA[("n1", i, t)])
        elif name == "post1":
            norm_post2(stA[("n1", i, t)], gmix, "gmix", xnT3[i % 2][:, :, t * 128:(t + 1) * 128],
                       ["x3_%d_%d" % (i, t)])
        elif name == "act2":
            stA[("n2", i, t)] = nsb_act(x_mid[:, T, :], "xm%d" % T, xb2_ring)
        elif name == "dve2":
            stA[("n2", i, t)] = nsb_dve(stA[("n2", i, t)])
        else:
            norm_post2(stA[("n2", i, t)], gmlp, "gmlp", xn2T_all[:, :, T * 128:(T + 1) * 128],
                       ["xn2T_%d" % T, "ya_pre%d" % i, "O_n%d" % i])

    def run_evA(j):
        evs = evA.get(j, [])
        for nm in ("ldx", "post1", "post2", "dve1", "dve2", "act1", "act2"):
            for (name, i, t) in evs:
                if name == nm:
                    do_evA(name, i, t)

    for i in range(1, 4):
        for t in range(4):
            if (i, t) != (1, 0):
                addA(max(8 * (i - 1) + t - 3, 0), ("ldx", i, t))
    for j, fn in enumerate(flatA):
        fn()
        if j == 0:
            for t in range(4):
                load_x(t, after=["Wg0", "Wg1"])
            load_x(4)
            run_evA(-1)
        run_evA(j)
    for j in range(len(flatA), len(flatA) + 4):
        run_evA(j)
    W1_pre = [(Wa.rearrange("p a b -> p (a b)").rearrange("p (c n) -> p c n", c=8), "Wa"),
              (Wb.rearrange("p a b -> p (a b)").rearrange("p (c n) -> p c n", c=8), "Wb")]
    for g_ in range(2):
        wload(W1_pre[g_][0], w_cols(w_1, 512 * g_, 512), W1_pre[g_][1])
    S.barrier()

    ar = Arena(big, B_BASE, TOP)
    h2g = [ar.take([128, 4, 2048], BF16) for _ in range(2)]
    W1_ring = Ring([W1_pre[0][0], W1_pre[1][0]], "W1", keys=["Wa", "Wb"])
    W2_ring = Ring([ar.take([128, 4, 1024], BF16) for _ in range(2)], "W2")
    r_ring = Ring([ar.take([128, 512], BF16) for _ in range(4)], "relu")
    xbf_ring = Ring([ar.take([128, 1024], BF16) for _ in range(2)], "xbf")
    mm_ring = Ring([bank(b) for b in range(0, 4)], "pmm", keys=["pb%d" % b for b in range(0, 4)])
    o_ring = Ring([bank(b) for b in range(4, 8)], "po", keys=["pb%d" % b for b in range(4, 8)])

    fin_pending = []

    def final_act(T):
        st, skey = stat_ring.next()
        xb, xbkey = xbf_ring.next()
        S.op("dve", lambda e: e.memset(st[:, 0:1], 0.0), writes=[skey])
        S.op("act", lambda e: e.activation(xb, x_mid[:, T, :], AF.Square, accum_out=st[:, 0:1]),
             reads=["xm%d" % T, skey], writes=[xbkey, skey])
        S.op("act", lambda e: e.activation(st[:, 1:2], st[:, 0:1], AF.Ln, bias=epsc[:, 0:1],
                                           scale=1.0 / D_MODEL), reads=[skey, "epsc"], writes=[skey])
        S.op("act", lambda e: e.activation(st[:, 2:3], st[:, 1:2], AF.Exp, scale=-0.5),
             reads=[skey], writes=[skey])
        return T, st, skey

    def final_dve_store(info):
        T, st, skey = info
        if T % 2:
            S.op("act", lambda e: e.activation(x_mid[:, T, :], x_mid[:, T, :], AF.Copy, scale=st[:, 2:3]),
                 reads=["xm%d" % T, skey], writes=["xm%d" % T])
            S.op("pool", lambda e: e.tensor_tensor(x_mid[:, T, :], x_mid[:, T, :], gfin, ALU.mult),
                 reads=["xm%d" % T, "gfin"], writes=["xm%d" % T])
        else:
            S.op("dve", lambda e: e.scalar_tensor_tensor(x_mid[:, T, :], x_mid[:, T, :], st[:, 2:3], gfin,
                                                         ALU.mult, ALU.mult),
                 reads=["xm%d" % T, skey, "gfin"], writes=["xm%d" % T])
        S.dma("sp", lambda e: e.dma_start(out=out_d[T * 128:(T + 1) * 128, :], in_=x_mid[:, T, :]),
              reads=["xm%d" % T])

    def mlp_in(g):
        w1t, w1k = W1_ring.next()
        if g >= 2:
            wload(w1t, w_cols(w_1, 512 * g, 512), w1k)
        hb = h2g[g % 2]
        for sl in range(4):
            xk2 = ["xn2T_%d" % (4 * sl + t) for t in range(4)]
            for fc in range(4):
                ph, phk = mm_ring.next()
                for c in range(8):
                    mm(ph, w1t[:, c, fc * 128:(fc + 1) * 128], xn2T_all[:, c, sl * 512:(sl + 1) * 512],
                       c == 0, c == 7, reads=[w1k] + xk2, writes=[phk])
                r, rk = r_ring.next()
                S.op("act", lambda e, r=r, ph=ph: e.activation(r, ph, AF.Relu), reads=[phk], writes=[rk])
                S.op("dve", lambda e, r=r, fc=fc, sl=sl: e.tensor_tensor(hb[:, fc, sl * 512:(sl + 1) * 512], r, r, ALU.mult),
                     reads=[rk], writes=["h2_%d_%d_%d" % (g % 2, fc, sl)])

    def mlp_out(g, final):
        w2t, w2k = W2_ring.next()
        wload(w2t, w_2[512 * g:512 * (g + 1), :].rearrange("(c p) n -> p c n", p=128), w2k)
        hb = h2g[g % 2]
        for T in range(16):
            for half in range(2):
                pz, pzk = o_ring.next()
                for fc in range(4):
                    mm(pz, hb[:, fc, T * 128:(T + 1) * 128], w2t[:, fc, half * 512:(half + 1) * 512],
                       fc == 0, fc == 3, reads=["h2_%d_%d_%d" % (g % 2, fc, T // 4), w2k], writes=[pzk])
                S.op("dve", lambda e, T=T, half=half, pz=pz: e.tensor_tensor(
                    x_mid[:, T, half * 512:(half + 1) * 512], pz, x_mid[:, T, half * 512:(half + 1) * 512], ALU.add),
                     reads=[pzk, "xm%d" % T], writes=["xm%d" % T])
            if final:
                fin_pending.append(final_act(T))
                if len(fin_pending) > 2:
                    final_dve_store(fin_pending.pop(0))
        if final:
            while fin_pending:
                final_dve_store(fin_pending.pop(0))

    mlp_in(0)
    for g in range(1, 8):
        mlp_in(g)
        mlp_out(g - 1, False)
    mlp_out(7, True)

    S.emit()
    return nc


def _core_layout(core):
    b = core // 2
    own = OWN_A if core % 2 == 0 else OWN_B
    other = OWN_B if core % 2 == 0 else OWN_A
    return b, own, other, own + other


def _bias_table(own, perm):
    part = np.arange(128, dtype=np.float64)
    tab = np.zeros((128, 4, 8, 4, 4), dtype=np.float64)
    for i in range(4):
        qmid = own[i] * 512 + 256
        for p in range(8):
            for j in range(4):
                kpos = perm[p] * 512 + j * 128 + part
                for h in range(4):
                    if perm[p] > own[i]:
                        tab[:, i, p, j, h] = NEG
                    else:
                        tab[:, i, p, j, h] = SLOPES[h] * (kpos - qmid)
    return tab.reshape(128, 512).astype(np.float32)


def make_in_maps(inputs):
    x = np.asarray(inputs["x"], dtype=np.float32)
    f = lambda a: np.ascontiguousarray(np.asarray(a, dtype=np.float32))
    w_in = f(inputs["w_in"][0])
    common = {
        "w_in": w_in,
        "w_a": f(inputs["w_a_out"][0]),
        "w_b": f(inputs["w_b_out"][0]),
        "w_o": f(inputs["w_o"][0]),
        "w_1": f(inputs["w_mlp_in"][0]),
        "w_2": f(inputs["w_mlp_out"][0]),
        "gmix": f(np.asarray(inputs["norm_mix_g"][0]).reshape(8, 128).T),
        "gmlp": f(np.asarray(inputs["norm_mlp_g"][0]).reshape(8, 128).T),
        "bgate": f(np.asarray(inputs["b_gate"][0]).reshape(16, 128).T),
        "convw": f(np.asarray(inputs["conv_w"][0]).reshape(3, 4, 128).transpose(2, 1, 0).reshape(128, 12)),
        "gsub": f(np.asarray(inputs["subln_g"][0]).reshape(128, 1)),
        "lam": f(np.concatenate([np.asarray(inputs[k][0]) for k in
                                 ("lambda_q1", "lambda_k1", "lambda_q2", "lambda_k2")]).reshape(1, 256)),
        "gfin": f(np.asarray(inputs["norm_final_g"]).reshape(1, 1024)),
        "ident": np.eye(128, dtype=np.float32).astype(ml_dtypes.bfloat16),
        "tri": np.where(np.arange(128)[:, None] > np.arange(128)[None, :], NEG, 0.0).astype(np.float32).astype(
            ml_dtypes.bfloat16),
    }
    maps = []
    for core in range(8):
        b, own, other, perm = _core_layout(core)
        xs = np.concatenate([x[b, nb * 512:(nb + 1) * 512] for nb in perm], axis=0)
        xh = np.zeros((8, 1024), dtype=np.float32)
        for i, nb in enumerate(own):
            if nb > 0:
                xh[2 * i:2 * i + 2] = x[b, nb * 512 - 2:nb * 512]
        m = dict(common)
        m["xs"] = np.ascontiguousarray(xs)
        m["xh"] = xh
        m["btab"] = _bias_table(own, perm)
        maps.append(m)
    return maps


_NC_CACHE = {}


def kernel(**inputs):
    if "nc" not in _NC_CACHE:
        _NC_CACHE["nc"] = build_program()
    nc = _NC_CACHE["nc"]
    in_maps = make_in_maps(inputs)
    res = run_bass_kernel_spmd(nc, in_maps, core_ids=list(range(8)))
    out = np.zeros((BATCH, SEQ, D_MODEL), dtype=np.float32)
    for core in range(8):
        b, own, other, perm = _core_layout(core)
        o = np.asarray(res.results[core]["out"], dtype=np.float32)
        for i, nb in enumerate(own):
            out[b, nb * 512:(nb + 1) * 512] = o[i * 512:(i + 1) * 512]
    return out
```

```python
import math
import numpy as np
import ml_dtypes
import concourse.bass as bass
import concourse.mybir as mybir
from concourse.bass_utils import run_bass_kernel_spmd

F32 = mybir.dt.float32
BF16 = mybir.dt.bfloat16
AF = mybir.ActivationFunctionType
ALU = mybir.AluOpType
AX = mybir.AxisListType

ENGS = ("pe", "act", "dve", "pool", "sp")
N_DMA_SEMS = 12

D_MODEL = 1024
SEQ = 4096
BATCH = 4
BLK = 512
NEG = -30000.0
SLOPES = [2.0 ** (-8.0 * (h + 1) / 4) for h in range(4)]
LAM_INIT = 0.8 - 0.6 * math.exp(-0.3 * 0)
EPS = 1e-6
OWN_A = [0, 3, 4, 7]
OWN_B = [1, 2, 5, 6]
LSUM_LAG = 3


class Sched:
    def __init__(self, nc):
        self.nc = nc
        self.ops = {e: [] for e in ENGS}
        self.last_writer = {}
        self.readers = {}
        self.dma_count = {e: 0 for e in ENGS}
        self.last_real = {e: None for e in ENGS}

    def _add(self, eng, fn, reads, writes, is_dma):
        writers = set()
        for k in reads:
            assert k in self.last_writer, ("read of a buffer nobody has written yet", k)
        for k in list(reads) + list(writes):
            w = self.last_writer.get(k)
            if w is not None:
                writers.add(w)
        wars = set()
        for k in writes:
            for r in self.readers.get(k, ()):
                wars.add(r)
        deps = set()
        for d in writers:
            if (not is_dma) and d[0] == "c" and d[1] == eng and eng == "pe":
                continue
            deps.add(d)
        for d in wars:
            if (not is_dma) and d[0] == "c" and d[1] == eng:
                continue
            deps.add(d)
        idx = len(self.ops[eng])
        if is_dma:
            ev = ("d", eng, self.dma_count[eng])
            self.dma_count[eng] += 1
        else:
            ev = ("c", eng, idx)
            self.last_real[eng] = ev
        self.ops[eng].append(dict(fn=fn, deps=deps, ev=ev, is_dma=is_dma))
        for k in writes:
            self.last_writer[k] = ev
            self.readers[k] = []
        for k in reads:
            self.readers.setdefault(k, []).append(ev)
        return ev

    def op(self, eng, fn, reads=(), writes=()):
        return self._add(eng, fn, reads, writes, False)

    def dma(self, eng, fn, reads=(), writes=()):
        return self._add(eng, fn, reads, writes, True)

    def barrier(self):
        deps = set()
        for e in ENGS:
            if self.last_real[e] is not None:
                deps.add(self.last_real[e])
            n = self.dma_count[e]
            for j in range(max(0, n - N_DMA_SEMS), n):
                deps.add(("d", e, j))
        for e in ENGS:
            self.ops[e].append(dict(fn=None, deps=set(deps), ev=None, is_dma=False))

    def emit(self):
        nc = self.nc
        waited = {e: set() for e in ENGS}
        for e in ENGS:
            for o in self.ops[e]:
                for d in o["deps"]:
                    if d[0] == "c":
                        waited[d[1]].add(d[2])
        csem = {e: nc.alloc_semaphore("c_" + e) for e in ENGS}
        dsem = {e: [nc.alloc_semaphore("d_%s_%d" % (e, i)) for i in range(N_DMA_SEMS)]
                for e in ENGS if self.dma_count[e] > 0}
        cval = {e: {} for e in ENGS}
        for e in ENGS:
            c = 0
            for i, o in enumerate(self.ops[e]):
                if o["fn"] is not None and (not o["is_dma"]) and i in waited[e]:
                    c += 1
                    cval[e][i] = c
        self.n_waits = 0

        def sem_val(d):
            if d[0] == "c":
                return csem[d[1]], cval[d[1]][d[2]]
            return dsem[d[1]][d[2] % N_DMA_SEMS], 16 * (d[2] // N_DMA_SEMS + 1)

        def run(e, eng):
            seen = {}
            for i, o in enumerate(self.ops[e]):
                deps = set(o["deps"])
                if o["is_dma"]:
                    j = o["ev"][2]
                    if j >= N_DMA_SEMS:
                        deps.add(("d", e, j - N_DMA_SEMS))
                for d in sorted(deps):
                    s, v = sem_val(d)
                    key = id(s)
                    if seen.get(key, 0) >= v:
                        continue
                    seen[key] = v
                    eng.wait_ge(s, v)
                    self.n_waits += 1
                if o["fn"] is None:
                    continue
                ins = o["fn"](eng)
                if o["is_dma"]:
                    s, v = sem_val(o["ev"])
                    ins.then_inc(s, 16)
                elif i in cval[e]:
                    ins.then_inc(csem[e], 1)
            n = self.dma_count[e]
            for k in range(N_DMA_SEMS):
                cnt = (n - k + N_DMA_SEMS - 1) // N_DMA_SEMS if n > k else 0
                if cnt > 0:
                    eng.wait_ge(dsem[e][k], 16 * cnt)

        with nc.Block() as block:
            @block.tensor
            def _(eng):
                run("pe", eng)

            @block.scalar
            def _(eng):
                run("act", eng)

            @block.vector
            def _(eng):
                run("dve", eng)

            @block.gpsimd
            def _(eng):
                run("pool", eng)

            @block.sync
            def _(eng):
                run("sp", eng)


class Ring:
    def __init__(self, aps, name, keys=None):
        self.aps = aps
        self.keys = keys if keys is not None else ["%s%d" % (name, k) for k in range(len(aps))]
        self.i = 0

    def next(self):
        k = self.i % len(self.aps)
        self.i += 1
        return self.aps[k], self.keys[k]


class Arena:
    def __init__(self, big, base, limit):
        self.big = big
        self.p = base
        self.limit = limit

    def take(self, shape, dt):
        n = 1
        for s in shape[1:]:
            n *= s
        nbytes = n * (4 if dt == F32 else 2)
        nbytes = (nbytes + 31) // 32 * 32
        off = self.p
        self.p += nbytes
        assert self.p <= self.limit, ("SBUF arena overflow", self.p, self.limit)
        ap = self.big[:, off // 2: (off + nbytes) // 2]
        if dt == F32:
            ap = ap.bitcast(F32)
        ap = ap[:, 0:n]
        if len(shape) == 3:
            ap = ap.rearrange("p (a b) -> p a b", a=shape[1])
        elif len(shape) == 4:
            ap = ap.rearrange("p (a b c) -> p a b c", a=shape[1], b=shape[2])
        return ap


def build_program(debug=False):
    nc = bass.Bass("TRN2", target_bir_lowering=False)

    def din(name, shape, dt=F32):
        return nc.dram_tensor(name, shape, dt, kind="ExternalInput").ap()

    xs = din("xs", [SEQ, D_MODEL])
    xh = din("xh", [8, D_MODEL])
    w_in = din("w_in", [D_MODEL, 5120])
    w_a = din("w_a", [512, D_MODEL])
    w_b = din("w_b", [512, D_MODEL])
    w_o = din("w_o", [D_MODEL, D_MODEL])
    w_1 = din("w_1", [D_MODEL, 4096])
    w_2 = din("w_2", [4096, D_MODEL])
    gmix_d = din("gmix", [128, 8])
    gmlp_d = din("gmlp", [128, 8])
    bgate_d = din("bgate", [128, 16])
    convw_d = din("convw", [128, 12])
    gsub_d = din("gsub", [128, 1])
    lam_d = din("lam", [1, 256])
    gfin_d = din("gfin", [1, D_MODEL])
    btab_d = din("btab", [128, 512])
    ident_d = din("ident", [128, 128], BF16)
    tri_d = din("tri", [128, 128], BF16)
    out_d = nc.dram_tensor("out", [2048, D_MODEL], F32, kind="ExternalOutput").ap()
    dbg = {}
    if debug:
        dbg["KT"] = nc.dram_tensor("dbg_KT", [128, 4 * SEQ], BF16, kind="ExternalOutput").ap()
        dbg["V"] = nc.dram_tensor("dbg_V", [128, 32 * 512], BF16, kind="ExternalOutput").ap()
        dbg["QT"] = nc.dram_tensor("dbg_QT", [128, 4 * 2048], BF16, kind="ExternalOutput").ap()
        dbg["YA"] = nc.dram_tensor("dbg_YA", [128, 4 * 2048], BF16, kind="ExternalOutput").ap()
        dbg["ON"] = nc.dram_tensor("dbg_ON", [128, 4 * 2048], BF16, kind="ExternalOutput").ap()

    S = Sched(nc)
    TOT = 212000
    big = nc.alloc_sbuf_tensor("big", [128, TOT // 2], BF16)
    pers = Arena(big, 0, TOT)

    ident = pers.take([128, 128], BF16)
    tri = pers.take([128, 128], BF16)
    ones_bf = pers.take([128, 128], BF16)
    onesdiv = pers.take([128, 128], F32)
    btab = pers.take([128, 512], F32)
    gmix = pers.take([128, 8], F32)
    gmlp = pers.take([128, 8], F32)
    bgate = pers.take([128, 16], F32)
    convw = pers.take([128, 12], F32)
    gsub = pers.take([128, 8], F32)
    lamv = pers.take([128, 256], F32)
    lamw = pers.take([128, 128], F32)
    lams = pers.take([128, 8], F32)
    epsc = pers.take([128, 8], F32)
    onec = pers.take([128, 8], F32)
    nbgate = pers.take([128, 16], F32)
    gfin = pers.take([128, 1024], F32)
    hhalo = pers.take([128, 4, 8], F32)
    stats = [pers.take([128, 8], F32) for _ in range(6)]
    stat_ring = Ring(stats, "stat")
    YA_OFF = pers.p
    ya_pre = pers.take([128, 4, 2048], BF16)
    ON_OFF = pers.p
    O_n = pers.take([128, 4, 2048], BF16)
    P3_BASE = pers.p
    QT = pers.take([128, 4, 2048], BF16)
    KT = pers.take([128, 4, SEQ], BF16)
    Vt = pers.take([128, 32, 512], BF16)
    SCR_BASE = pers.p
    TOP = TOT - 32768
    KEEP_OFF = TOP - 8192
    keepA = Arena(big, KEEP_OFF, TOP)
    xnT_keep = keepA.take([128, 8, 512], BF16)
    top3 = Arena(big, TOP, TOT)
    Wa = top3.take([128, 4, 1024], BF16)
    Wb = top3.take([128, 4, 1024], BF16)
    Wo = top3.take([128, 8, 1024], BF16)

    ps_all = nc.alloc_psum_tensor("ps_all", [128, 8 * 512], F32)

    def bank(b, n=1):
        return ps_all[:, b * 512:(b + n) * 512]

    def bank_bf_T(b):
        return bank(b).bitcast(BF16).rearrange("p (c t) -> p c t", c=8)

    def ld(eng, dst, src, key):
        S.dma(eng, lambda e: e.dma_start(out=dst, in_=src), writes=[key])

    ld("sp", ident, ident_d, "ident")
    ld("sp", tri, tri_d, "tri")
    ld("sp", btab, btab_d, "btab")
    ld("sp", gmix, gmix_d, "gmix")
    ld("sp", gmlp, gmlp_d, "gmlp")
    ld("sp", bgate, bgate_d, "bgate")
    ld("sp", convw, convw_d, "convw")
    ld("sp", gsub[:, 0:1], gsub_d, "gsub0")
    ld("sp", lamv, lam_d.partition_broadcast(128), "lamv")
    ld("sp", gfin, gfin_d.partition_broadcast(128), "gfin")
    S.op("pool", lambda e: e.memset(ones_bf, 1.0), writes=["ones_bf"])
    S.op("pool", lambda e: e.memset(onesdiv, 1.0 / 128.0), writes=["onesdiv"])
    S.op("pool", lambda e: e.memset(epsc, EPS), writes=["epsc"])
    S.op("pool", lambda e: e.memset(onec, 1.0), writes=["onec"])
    S.op("dve", lambda e: e.tensor_scalar_mul(nbgate, bgate, -1.0), reads=["bgate"], writes=["nbgate"])
    S.op("dve", lambda e: e.tensor_tensor(lamw[:, 0:64], lamv[:, 0:64], lamv[:, 64:128], ALU.mult),
         reads=["lamv"], writes=["lamw"])
    S.op("dve", lambda e: e.tensor_tensor(lamw[:, 64:128], lamv[:, 128:192], lamv[:, 192:256], ALU.mult),
         reads=["lamv"], writes=["lamw"])
    S.op("dve", lambda e: e.reduce_sum(lams[:, 0:1], lamw[:, 0:64], axis=AX.X), reads=["lamw"], writes=["lams"])
    S.op("dve", lambda e: e.reduce_sum(lams[:, 1:2], lamw[:, 64:128], axis=AX.X), reads=["lamw"], writes=["lams"])
    S.op("act", lambda e: e.activation(lams[:, 2:4], lams[:, 0:2], AF.Exp), reads=["lams"], writes=["lams"])
    S.op("dve", lambda e: e.tensor_tensor(lams[:, 4:5], lams[:, 3:4], lams[:, 2:3], ALU.subtract),
         reads=["lams"], writes=["lams"])
    S.op("dve", lambda e: e.tensor_scalar_add(lams[:, 5:6], lams[:, 4:5], -LAM_INIT), reads=["lams"], writes=["lams"])
    neg_lam = lams[:, 5:6]
    S.op("dve", lambda e: e.tensor_scalar_mul(gsub[:, 1:2], gsub[:, 0:1], 1.0 - LAM_INIT), reads=["gsub0"], writes=["gsub"])

    def wload(dst, src, key):
        S.dma("pool", lambda e: e.dma_start(out=dst, in_=src), writes=[key])

    def w_cols(w, c0, n):
        return w[:, c0:c0 + n].rearrange("(c p) n -> p c n", p=128)

    def norm_T(xt, xkey, ntok, gcols, gkey, dst, dkey, xb_ring, pst_ring):
        st, skey = stat_ring.next()
        xb, xbkey = xb_ring.next()
        S.op("dve", lambda e: e.memset(st[:ntok, 0:1], 0.0), writes=[skey])
        S.op("act", lambda e: e.activation(xb[:ntok], xt[:ntok], AF.Square, accum_out=st[:ntok, 0:1]),
             reads=[xkey, skey], writes=[xbkey, skey])
        S.op("act", lambda e: e.activation(st[:ntok, 1:2], st[:ntok, 0:1], AF.Ln, bias=epsc[:ntok, 0:1],
                                           scale=1.0 / D_MODEL), reads=[skey, "epsc"], writes=[skey])
        S.op("act", lambda e: e.activation(st[:ntok, 2:3], st[:ntok, 1:2], AF.Exp, scale=-0.5),
             reads=[skey], writes=[skey])
        S.op("dve", lambda e: e.tensor_scalar_mul(xb[:ntok], xt[:ntok], st[:ntok, 2:3]),
             reads=[xkey, skey], writes=[xbkey])
        pst, pkey = pst_ring.next()
        for c in range(8):
            S.op("pe", lambda e, c=c: e.transpose(pst[:, c, :ntok], xb[:ntok, c * 128:(c + 1) * 128],
                                                   ident[:ntok, :ntok]),
                 reads=[xbkey, "ident"], writes=[pkey])
        S.op("dve", lambda e: e.tensor_tensor(dst, pst[:, :, :ntok],
                                              gcols.unsqueeze(2).to_broadcast([128, 8, ntok]), ALU.mult),
             reads=[pkey, gkey], writes=[dkey])
        return st, skey

    def mm(out, lhsT, rhs, start, stop, reads, writes, tp=None):
        if tp is None:
            S.op("pe", lambda e: e.matmul(out, lhsT, rhs, start=start, stop=stop), reads=reads, writes=writes)
        else:
            S.op("pe", lambda e: e.matmul(out, lhsT, rhs, start=start, stop=stop, tile_position=tp),
                 reads=reads, writes=writes)

    def dump(name, src, key):
        if debug:
            S.dma("sp", lambda e: e.dma_start(out=dbg[name], in_=src), reads=[key])

    def npre_dma(src_rows, ntok, xt_ring):
        xt, xkey = xt_ring.next()
        S.dma("sp", lambda e: e.dma_start(out=xt[:ntok], in_=src_rows), writes=[xkey])
        return xt, xkey

    def npre_act(xt, xkey, ntok, xb_ring):
        st, skey = stat_ring.next()
        xb, xbkey = xb_ring.next()
        S.op("dve", lambda e: e.memset(st[:ntok, 0:1], 0.0), writes=[skey])
        S.op("act", lambda e: e.activation(xb[:ntok], xt[:ntok], AF.Square, accum_out=st[:ntok, 0:1]),
             reads=[xkey, skey], writes=[xbkey, skey])
        S.op("act", lambda e: e.activation(st[:ntok, 1:2], st[:ntok, 0:1], AF.Ln, bias=epsc[:ntok, 0:1],
                                           scale=1.0 / D_MODEL), reads=[skey, "epsc"], writes=[skey])
        S.op("act", lambda e: e.activation(st[:ntok, 2:3], st[:ntok, 1:2], AF.Exp, scale=-0.5),
             reads=[skey], writes=[skey])
        return xt, xkey, xb, xbkey, st, skey

    def npre_dve(info, ntok):
        xt, xkey, xb, xbkey, st, skey = info
        S.op("dve", lambda e: e.tensor_scalar_mul(xb[:ntok], xt[:ntok], st[:ntok, 2:3]),
             reads=[xkey, skey], writes=[xbkey])
        return xb, xbkey

    def norm_pre(src_rows, ntok, xt_ring, xb_ring):
        xt, xkey = npre_dma(src_rows, ntok, xt_ring)
        return npre_dve(npre_act(xt, xkey, ntok, xb_ring), ntok)

    def norm_post(xb, xbkey, ntok, gcols, gkey, dst, dkey, pst_ring):
        pst, pkey = pst_ring.next()
        for c in range(8):
            S.op("pe", lambda e, c=c: e.transpose(pst[:, c, :ntok], xb[:ntok, c * 128:(c + 1) * 128],
                                                   ident[:ntok, :ntok]),
                 reads=[xbkey, "ident"], writes=[pkey])
        S.op("dve", lambda e: e.tensor_tensor(dst, pst[:, :, :ntok],
                                              gcols.unsqueeze(2).to_broadcast([128, 8, ntok]), ALU.mult),
             reads=[pkey, gkey], writes=[dkey])

    evac_flip = [0]

    def evac(dst, src, reads, writes):
        evac_flip[0] ^= 1
        if evac_flip[0]:
            S.op("act", lambda e: e.copy(dst, src), reads=reads, writes=writes)
        else:
            S.op("dve", lambda e: e.tensor_copy(dst, src), reads=reads, writes=writes)

    arO = Arena(big, 0, TOT)
    arO.p = ON_OFF
    xb_ring = Ring([arO.take([128, 1024], BF16) for _ in range(4)], "xb")
    cg_ring = Ring([arO.take([128, 512], F32) for _ in range(1)], "cg")
    hb_ring = Ring([arO.take([128, 516], F32) for _ in range(1)], "hb")
    acc_ring = Ring([arO.take([128, 512], F32) for _ in range(1)], "cacc")
    assert arO.p <= ON_OFF + 16384
    ar = Arena(big, SCR_BASE, KEEP_OFF)
    Wc = ar.take([128, 8, 1536], BF16)
    Wk = ar.take([128, 8, 512], BF16)
    Wv = ar.take([128, 8, 512], BF16)
    xt_a = ar.take([128, 1024], F32)
    xnTh = ar.take([128, 8, 8], BF16)
    ar2 = Arena(big, TOP, TOT)
    Wq = ar2.take([128, 8, 512], BF16)
    xnT_bufs = [ar2.take([128, 8, 512], BF16) for _ in range(2)]
    xt_ring = Ring([xt_a] + [ar2.take([128, 1024], F32) for _ in range(2)], "xt")
    pst_ring = Ring([bank_bf_T(0), bank_bf_T(1)], "pst", keys=["pb0", "pb1"])
    mm_ring = Ring([bank(b) for b in range(2, 8)], "pmm", keys=["pb%d" % b for b in range(2, 8)])

    wload(Wv, w_cols(w_in, 2560, 512), "Wv")
    wload(Wk, w_cols(w_in, 2048, 512), "Wk")

    def wload_after(dst, src, key, after):
        S.dma("pool", lambda e: e.dma_start(out=dst, in_=src), reads=after, writes=[key])

    order = [4, 5, 6, 7, 0, 1, 2, 3]

    def xk_of(blk):
        return ["xn%d_%d" % (blk, t) for t in range(4)]

    def kpart(blk, xnT, hs):
        def f():
            for h in hs:
                pb, pkey = mm_ring.next()
                for c in range(8):
                    mm(pb, Wk[:, c, h * 128:(h + 1) * 128], xnT[:, c, :], c == 0, c == 7,
                       reads=["Wk"] + xk_of(blk), writes=[pkey])
                S.op("act", lambda e, pb=pb, h=h: e.copy(KT[:, h, blk * 512:(blk + 1) * 512], pb),
                     reads=[pkey], writes=["KT%d" % blk])
        return f

    def vpart(blk, xnT, ts):
        def f():
            for t in ts:
                pb, pkey = mm_ring.next()
                for c in range(8):
                    mm(pb, xnT[:, c, t * 128:(t + 1) * 128], Wv[:, c, :], c == 0, c == 7,
                       reads=["Wv", xk_of(blk)[t]], writes=[pkey])
                evac(Vt[:, blk * 4 + t, :], pb, [pkey], ["V%d" % (blk * 4 + t)])
        return f

    def qpart(i, xnT):
        cols = slice(i * 512, (i + 1) * 512)

        def f():
            for h in range(4):
                pb, pkey = mm_ring.next()
                for c in range(8):
                    mm(pb, Wq[:, c, h * 128:(h + 1) * 128], xnT[:, c, :], c == 0, c == 7,
                       reads=["Wq"] + xk_of(i), writes=[pkey])
                evac(QT[:, h, cols], pb, [pkey], ["QT%d" % i])
        return f

    def cpart(i, xnT, ch):
        cols = slice(i * 512, (i + 1) * 512)
        xkeys = xk_of(i)

        def f():
            pc, pckey = mm_ring.next()
            for c in range(8):
                mm(pc, Wc[:, c, 512 + ch * 128:512 + (ch + 1) * 128], xnT[:, c, :], c == 0, c == 7,
                   reads=["Wc_cv"] + xkeys, writes=[pckey])
            pv, pvkey = mm_ring.next()
            for c in range(8):
                mm(pv, Wc[:, c, 1024 + ch * 128:1024 + (ch + 1) * 128], xnT[:, c, :], c == 0, c == 7,
                   reads=["Wc_cv"] + xkeys, writes=[pvkey])
            pg, pgkey = mm_ring.next()
            for c in range(8):
                mm(pg, Wc[:, c, ch * 128:(ch + 1) * 128], xnT[:, c, :], c == 0, c == 7,
                   reads=["Wc_b"] + xkeys, writes=[pgkey])
            cg, cgkey = cg_ring.next()
            hb, hbkey = hb_ring.next()
            acc, acckey = acc_ring.next()
            S.op("act", lambda e: e.copy(cg, pc), reads=[pckey], writes=[cgkey])
            S.op("dve", lambda e: e.tensor_copy(hb[:, 0:2], hhalo[:, ch, 2 * i:2 * i + 2]),
                 reads=["hhalo"], writes=[hbkey])
            S.op("dve", lambda e: e.tensor_tensor(hb[:, 2:514], pv, cg, ALU.mult),
                 reads=[pvkey, cgkey], writes=[hbkey])
            S.op("dve", lambda e: e.tensor_scalar_mul(acc, hb[:, 2:514], convw[:, ch * 3 + 2:ch * 3 + 3]),
                 reads=[hbkey, "convw"], writes=[acckey])
            S.op("dve", lambda e: e.scalar_tensor_tensor(
                acc, hb[:, 1:513], convw[:, ch * 3 + 1:ch * 3 + 2], acc, ALU.mult, ALU.add),
                 reads=[hbkey, "convw", acckey], writes=[acckey])
            S.op("dve", lambda e: e.scalar_tensor_tensor(
                acc, hb[:, 0:512], convw[:, ch * 3:ch * 3 + 1], acc, ALU.mult, ALU.add),
                 reads=[hbkey, "convw", acckey], writes=[acckey])
            S.op("dve", lambda e: e.tensor_tensor(ya_pre[:, ch, cols], pg, acc, ALU.mult),
                 reads=[pgkey, acckey], writes=["ya_pre%d" % i])
        return f

    def halo_part():
        xb, xbkey = norm_pre(xh, 8, xt_ring, xb_ring)
        norm_post(xb, xbkey, 8, gmix, "gmix", xnTh, "xnTh", pst_ring)
        for ch in range(4):
            pc, pckey = mm_ring.next()
            for c in range(8):
                mm(pc[:, 0:8], Wc[:, c, 512 + ch * 128:512 + (ch + 1) * 128], xnTh[:, c, :], c == 0, c == 7,
                   reads=["Wc_cv", "xnTh"], writes=[pckey])
            pv, pvkey = mm_ring.next()
            for c in range(8):
                mm(pv[:, 0:8], Wc[:, c, 1024 + ch * 128:1024 + (ch + 1) * 128], xnTh[:, c, :], c == 0, c == 7,
                   reads=["Wc_cv", "xnTh"], writes=[pvkey])
            cg, cgkey = cg_ring.next()
            S.op("act", lambda e, cg=cg, pc=pc: e.copy(cg[:, 0:8], pc[:, 0:8]), reads=[pckey], writes=[cgkey])
            S.op("dve", lambda e, cg=cg, pv=pv, ch=ch: e.tensor_tensor(hhalo[:, ch, :], pv[:, 0:8], cg[:, 0:8], ALU.mult),
                 reads=[pvkey, cgkey], writes=["hhalo"])

    flat = []
    blk_last_part = {}
    def xnT_of(si):
        return xnT_keep if order[si] == 0 else xnT_bufs[si % 2]

    for si, blk in enumerate(order):
        xnT = xnT_of(si)
        parts = [vpart(blk, xnT, [0, 1]), vpart(blk, xnT, [2, 3]), kpart(blk, xnT, [0, 1]), kpart(blk, xnT, [2, 3])]
        if blk < 4:
            parts += [qpart(blk, xnT)] + [cpart(blk, xnT, ch) for ch in range(4)]
        if blk == 0:
            flat.append(halo_part)
        for p_ in parts:
            flat.append(p_)
        blk_last_part[si] = len(flat) - 1

    ev_at = {}
    for si in range(1, 8):
        for t in range(4):
            pp = blk_last_part[si - 1] - 3 + t
            for name, off in (("post", 0), ("dve", -2), ("act", -3), ("dma", -4)):
                ev_at.setdefault(max(pp + off, -1), []).append((name, si, t))
    st1 = {}

    def do_ev(name, si, t):
        blk = order[si]
        if name == "dma":
            r0 = blk * 512 + t * 128
            st1[(si, t)] = npre_dma(xs[r0:r0 + 128, :], 128, xt_ring)
        elif name == "act":
            xt, xkey = st1[(si, t)]
            st1[(si, t)] = npre_act(xt, xkey, 128, xb_ring)
        elif name == "dve":
            st1[(si, t)] = npre_dve(st1[(si, t)], 128)
        else:
            xb, xbkey = st1[(si, t)]
            norm_post(xb, xbkey, 128, gmix, "gmix", xnT_of(si)[:, :, t * 128:(t + 1) * 128],
                      "xn%d_%d" % (blk, t), pst_ring)

    def run_evs(j):
        evs = ev_at.get(j, [])
        for nm in ("post", "dve", "act", "dma"):
            for (name, si, t) in evs:
                if name == nm:
                    do_ev(name, si, t)

    do_ev("dma", 0, 0)
    do_ev("dma", 0, 1)
    do_ev("dma", 0, 2)
    do_ev("act", 0, 0)
    do_ev("act", 0, 1)
    do_ev("dve", 0, 0)
    do_ev("dma", 0, 3)
    do_ev("act", 0, 2)
    do_ev("dve", 0, 1)
    do_ev("post", 0, 0)
    do_ev("act", 0, 3)
    do_ev("dve", 0, 2)
    do_ev("post", 0, 1)
    do_ev("dve", 0, 3)
    do_ev("post", 0, 2)
    do_ev("post", 0, 3)
    early = ev_at.get(-1, [])
    for (si_, t_) in sorted(set((si, t) for (_, si, t) in early)):
        for nm in ("dma", "act", "dve", "post"):
            if (nm, si_, t_) in early:
                do_ev(nm, si_, t_)
    for j, fn in enumerate(flat):
        fn()
        run_evs(j)
        if j == 0:
            wload_after(Wc[:, :, 512:1536], w_cols(w_in, 512, 1024), "Wc_cv", ["xn5_0"])
            wload_after(Wq, w_cols(w_in, 1536, 512), "Wq", ["xn5_0"])
            wload_after(Wc[:, :, 0:512], w_cols(w_in, 0, 512), "Wc_b", ["xn5_0"])
    dump("KT", KT.rearrange("p a b -> p (a b)"), "KT3")
    dump("V", Vt.rearrange("p a b -> p (a b)"), "V15")
    dump("QT", QT.rearrange("p a b -> p (a b)"), "QT3")
    dump("YA", ya_pre.rearrange("p a b -> p (a b)"), "ya_pre3")
    S.barrier()

    wload(Wa, w_a.rearrange("(c p) n -> p c n", p=128), "Wa")
    wload(Wb, w_b.rearrange("(c p) n -> p c n", p=128), "Wb")
    wload(Wo, w_o.rearrange("(c p) n -> p c n", p=128), "Wo")
    ar = Arena(big, SCR_BASE, TOP)
    PT_ring = Ring([ar.take([128, 2, 512], BF16) for _ in range(8)], "PT")
    ectx = []
    for k in range(2):
        ectx.append(dict(Ocp=ar.take([128, 2, 512], F32), Lcp=ar.take([128, 2, 512], F32),
                         o=ar.take([128, 512], F32), sq=ar.take([128, 512], F32), ln=ar.take([128, 512], F32),
                         k="e%d" % k))
    Sps = ps_all[:, 0:2048].rearrange("p (b m n) -> p b m n", b=2, m=2)
    Ops = ps_all[:, 2048:3072].rearrange("p (m n) -> p m n", m=2)
    Lps = ps_all[:, 3072:4096].rearrange("p (m n) -> p m n", m=2)

    steps = []
    hid = 0
    for i in range(4):
        for h in range(4):
            plist = [i] + list(range(i)) + [4 + x for x in range(i + 1)]
            tiles = [(p, j) for p in plist for j in range(4)]
            for n, (p, j) in enumerate(tiles):
                steps.append(dict(i=i, h=h, p=p, j=j, cs=(128 * j if p == i else 0), diag=(p == i),
                                  first=(n == 0), last=(n == len(tiles) - 1), hid=hid))
            hid += 1

    s_cnt = [0]
    s_of = {}

    def take_s():
        k = s_cnt[0] % 2
        s_cnt[0] += 1
        return k

    def emit_qk(n):
        st = steps[n]
        i, h, p, j, cs = st["i"], st["h"], st["p"], st["j"], st["cs"]
        buf = take_s()
        s_of[n] = buf
        skey = "S%d" % buf
        k0 = p * 512 + j * 128
        q0 = i * 512
        if st["diag"]:
            for m in range(2):
                mm(Sps[:, buf, m, cs:cs + 128], KT[64 * m:64 * m + 64, h, k0:k0 + 128],
                   QT[64 * m:64 * m + 64, h, q0 + cs:q0 + cs + 128], True, False,
                   reads=["KT%d" % p, "QT%d" % i], writes=[skey], tp=(64 * m, 0))
            for m in range(2):
                mm(Sps[:, buf, m, cs:cs + 128], ident, tri, False, True, reads=["ident", "tri"], writes=[skey])
            if cs + 128 < 512:
                for m in range(2):
                    mm(Sps[:, buf, m, cs + 128:512], KT[64 * m:64 * m + 64, h, k0:k0 + 128],
                       QT[64 * m:64 * m + 64, h, q0 + cs + 128:q0 + 512], True, True,
                       reads=["KT%d" % p, "QT%d" % i], writes=[skey], tp=(64 * m, 0))
        else:
            for m in range(2):
                mm(Sps[:, buf, m, :], KT[64 * m:64 * m + 64, h, k0:k0 + 128],
                   QT[64 * m:64 * m + 64, h, q0:q0 + 512], True, True,
                   reads=["KT%d" % p, "QT%d" % i], writes=[skey], tp=(64 * m, 0))

    pt_of = {}
    pt_user = {}
    pv_done = set()
    ls_done = set()

    def emit_exp(n):
        st = steps[n]
        cs = st["cs"]
        buf = s_of[n]
        PT, ptkey = PT_ring.next()
        prev = pt_user.get(ptkey)
        assert prev is None or (prev in pv_done and prev in ls_done), ("PT ring too small", n, prev)
        pt_user[ptkey] = n
        pt_of[n] = (PT, ptkey)
        bidx = ((st["i"] * 8 + st["p"]) * 4 + st["j"]) * 4 + st["h"]
        S.op("act", lambda e: e.activation(PT[:, :, cs:512], Sps[:, buf, :, cs:512], AF.Exp,
                                           bias=btab[:, bidx:bidx + 1], scale=0.125),
             reads=["S%d" % buf, "btab"], writes=[ptkey])

    def emit_pv(n):
        st = steps[n]
        cs, h = st["cs"], st["h"]
        PT, ptkey = pt_of[n]
        kt = st["p"] * 4 + st["j"]
        pv_done.add(n)
        for m in range(2):
            mm(Ops[:, m, cs:512], Vt[:, kt, h * 128:(h + 1) * 128], PT[:, m, cs:512], st["first"], st["last"],
               reads=["V%d" % kt, ptkey], writes=["O"])

    def emit_lsum(n):
        st = steps[n]
        cs = st["cs"]
        PT, ptkey = pt_of[n]
        ls_done.add(n)
        for m in range(2):
            mm(Lps[:, m, cs:512], ones_bf, PT[:, m, cs:512], st["first"], st["last"],
               reads=["ones_bf", ptkey], writes=["L"])

    def emit_ocopy(n):
        c = ectx[steps[n]["hid"] % 2]
        S.op("dve", lambda e: e.tensor_copy(c["Ocp"], Ops), reads=["O"], writes=[c["k"] + "O"])

    def emit_chain(n):
        c = ectx[steps[n]["hid"] % 2]
        k = c["k"]
        S.op("dve", lambda e: e.tensor_copy(c["Lcp"], Lps), reads=["L"], writes=[k + "L"])
        S.op("dve", lambda e: e.reciprocal(c["Lcp"], c["Lcp"]), reads=[k + "L"], writes=[k + "L"])
        S.op("dve", lambda e: e.tensor_tensor(c["Ocp"], c["Ocp"], c["Lcp"], ALU.mult),
             reads=[k + "O", k + "L"], writes=[k + "O"])
        S.op("dve", lambda e: e.scalar_tensor_tensor(c["o"], c["Ocp"][:, 1, :], neg_lam, c["Ocp"][:, 0, :],
                                                     ALU.mult, ALU.add),
             reads=[k + "O", "lams"], writes=[k + "o"])
        S.op("dve", lambda e: e.tensor_tensor(c["sq"], c["o"], c["o"], ALU.mult), reads=[k + "o"], writes=[k + "sq"])

    def emit_ms(n):
        st = steps[n]
        c = ectx[st["hid"] % 2]
        k = c["k"]
        i, h = st["i"], st["h"]
        cols = slice(i * 512, (i + 1) * 512)
        buf = take_s()
        mm(Sps[:, buf, 0, :], onesdiv, c["sq"], True, True, reads=["onesdiv", k + "sq"], writes=["S%d" % buf])
        S.op("act", lambda e: e.activation(c["ln"], Sps[:, buf, 0, :], AF.Ln, bias=epsc[:, 0:1], scale=1.0),
             reads=["S%d" % buf, "epsc"], writes=[k + "ln"])
        S.op("act", lambda e: e.activation(c["ln"], c["ln"], AF.Exp, scale=-0.5), reads=[k + "ln"], writes=[k + "ln"])
        S.op("dve", lambda e: e.scalar_tensor_tensor(O_n[:, h, cols], c["o"], gsub[:, 1:2], c["ln"], ALU.mult, ALU.mult),
             reads=[k + "o", "gsub", k + "ln"], writes=["O_n%d" % i])

    NS = len(steps)
    emit_qk(0)
    lq = []
    hold_until = 0
    ms_due = []
    for n in range(NS + 1):
        if n < NS:
            emit_exp(n)
        while ms_due and ms_due[0][0] <= n:
            emit_ms(ms_due.pop(0)[1])
        if n + 1 < NS:
            emit_qk(n + 1)
        if n >= 1:
            emit_pv(n - 1)
            if steps[n - 1]["last"]:
                emit_ocopy(n - 1)
        if n < NS:
            lq.append(n)
        cnt = 0
        while lq and lq[0] <= n - LSUM_LAG and n >= hold_until and cnt < 2:
            q = lq.pop(0)
            emit_lsum(q)
            cnt += 1
            if steps[q]["last"]:
                emit_chain(q)
                hold_until = n + 3
                ms_due.append((n + 11, q))
                break
    while lq:
        q = lq.pop(0)
        emit_lsum(q)
        if steps[q]["last"]:
            emit_chain(q)
            ms_due.append((0, q))
    for _, q in ms_due:
        emit_ms(q)
    dump("ON", O_n.rearrange("p a b -> p (a b)"), "O_n3")
    S.barrier()

    ar = Arena(big, P3_BASE, KEEP_OFF)
    x_mid = ar.take([128, 16, 1024], F32)
    B_BASE = ar.p
    xnT3 = [xnT_keep, ar.take([128, 8, 512], BF16)]
    mergedT = ar.take([128, 8, 512], BF16)
    Wg_ring = Ring([ar.take([128, 8, 256], BF16) for _ in range(4)], "Wg")
    tf_ring = Ring([ar.take([128, 512], F32) for _ in range(6)], "tf")
    xb1_ring = Ring([ar.take([128, 1024], BF16) for _ in range(4)], "xb1_")
    xb2_ring = Ring([ar.take([128, 1024], BF16) for _ in range(4)], "xb2_")
    assert ON_OFF == YA_OFF + 16384
    xn2T_all = big[:, YA_OFF // 2: YA_OFF // 2 + 16384].rearrange("p (c t) -> p c t", c=8)
    pst_ring = Ring([bank_bf_T(6), bank_bf_T(7)], "pst", keys=["pb6", "pb7"])
    mm_ring = Ring([bank(b) for b in range(0, 4)], "pmm", keys=["pb%d" % b for b in range(0, 4)])
    z_ring = Ring([bank(b) for b in (4, 5)], "pz", keys=["pb4", "pb5"])

    def load_x(T, after=()):
        S.dma("sp", lambda e: e.dma_start(out=x_mid[:, T, :], in_=xs[T * 128:(T + 1) * 128, :]),
              reads=list(after), writes=["xm%d" % T])

    def nsb_act(xt, xkey, xb_ring):
        st, skey = stat_ring.next()
        xb, xbkey = xb_ring.next()
        S.op("dve", lambda e: e.memset(st[:, 0:1], 0.0), writes=[skey])
        S.op("act", lambda e: e.activation(xb, xt, AF.Square, accum_out=st[:, 0:1]),
             reads=[xkey, skey], writes=[xbkey, skey])
        S.op("act", lambda e: e.activation(st[:, 1:2], st[:, 0:1], AF.Ln, bias=epsc[:, 0:1],
                                           scale=1.0 / D_MODEL), reads=[skey, "epsc"], writes=[skey])
        S.op("act", lambda e: e.activation(st[:, 2:3], st[:, 1:2], AF.Exp, scale=-0.5),
             reads=[skey], writes=[skey])
        return xt, xkey, xb, xbkey, st, skey

    def nsb_dve(info):
        xt, xkey, xb, xbkey, st, skey = info
        S.op("dve", lambda e: e.tensor_scalar_mul(xb, xt, st[:, 2:3]), reads=[xkey, skey], writes=[xbkey])
        return xb, xbkey

    def norm_post2(xbinfo, gcols, gkey, dst, dkeys):
        xb, xbkey = xbinfo
        pst, pkey = pst_ring.next()
        for c in range(8):
            S.op("pe", lambda e, c=c: e.transpose(pst[:, c, :], xb[:, c * 128:(c + 1) * 128], ident),
                 reads=[xbkey, "ident"], writes=[pkey])
        S.op("dve", lambda e: e.tensor_tensor(dst, pst, gcols.unsqueeze(2).to_broadcast([128, 8, 128]), ALU.mult),
             reads=[pkey, gkey], writes=dkeys)

    def gates_step(i, k):
        cols = slice(i * 512, (i + 1) * 512)
        xn = xnT3[i % 2]
        xk3 = ["x3_%d_%d" % (i, t) for t in range(4)] if i > 0 else ["xn0_%d" % t for t in range(4)]

        def f():
            ga_t, gakey = Wg_ring.next()
            wload(ga_t, w_cols(w_in, 3072 + 256 * k, 256), gakey)
            gb_t, gbkey = Wg_ring.next()
            wload(gb_t, w_cols(w_in, 4096 + 256 * k, 256), gbkey)
            for cc in range(2):
                dm = 2 * k + cc
                pga, pgak = mm_ring.next()
                for c in range(8):
                    mm(pga, ga_t[:, c, cc * 128:(cc + 1) * 128], xn[:, c, :], c == 0, c == 7,
                       reads=[gakey] + xk3, writes=[pgak])
                pgb, pgbk = mm_ring.next()
                for c in range(8):
                    mm(pgb, gb_t[:, c, cc * 128:(cc + 1) * 128], xn[:, c, :], c == 0, c == 7,
                       reads=[gbkey] + xk3, writes=[pgbk])
                t1, t1k = tf_ring.next()
                t2, t2k = tf_ring.next()
                S.op("act", lambda e, t1=t1, pga=pga, dm=dm: e.activation(t1, pga, AF.Exp, bias=nbgate[:, dm:dm + 1],
                                                                        scale=-1.0),
                     reads=[pgak, "nbgate"], writes=[t1k])
                S.op("act", lambda e, t2=t2, pgb=pgb, dm=dm: e.activation(t2, pgb, AF.Exp, bias=nbgate[:, 8 + dm:9 + dm],
                                                                        scale=-1.0),
                     reads=[pgbk, "nbgate"], writes=[t2k])
                S.op("act", lambda e, t1=t1: e.activation(t1, t1, AF.Ln, bias=onec[:, 0:1], scale=1.0),
                     reads=[t1k, "onec"], writes=[t1k])
                S.op("act", lambda e, t2=t2: e.activation(t2, t2, AF.Ln, bias=onec[:, 0:1], scale=1.0),
                     reads=[t2k, "onec"], writes=[t2k])
                S.op("act", lambda e, t1=t1: e.activation(t1, t1, AF.Exp, scale=-1.0), reads=[t1k], writes=[t1k])
                S.op("act", lambda e, t2=t2: e.activation(t2, t2, AF.Exp, scale=-1.0), reads=[t2k], writes=[t2k])
                pya, pyak = mm_ring.next()
                for c in range(4):
                    mm(pya, Wa[:, c, dm * 128:(dm + 1) * 128], ya_pre[:, c, cols], c == 0, c == 3,
                       reads=["Wa", "ya_pre%d" % i], writes=[pyak])
                S.op("dve", lambda e, t1=t1, pya=pya: e.tensor_tensor(t1, pya, t1, ALU.mult),
                     reads=[pyak, t1k], writes=[t1k])
                pyb, pybk = mm_ring.next()
                for c in range(4):
                    mm(pyb, Wb[:, c, dm * 128:(dm + 1) * 128], O_n[:, c, cols], c == 0, c == 3,
                       reads=["Wb", "O_n%d" % i], writes=[pybk])
                S.op("dve", lambda e, t2=t2, pyb=pyb: e.tensor_tensor(t2, pyb, t2, ALU.mult),
                     reads=[pybk, t2k], writes=[t2k])
                S.op("dve", lambda e, t1=t1, t2=t2, dm=dm: e.tensor_tensor(mergedT[:, dm, :], t1, t2, ALU.add),
                     reads=[t1k, t2k], writes=["mT%d" % dm])
        return f

    def z_tile(i, t):
        T = 4 * i + t
        mkeys = ["mT%d" % dm for dm in range(8)]

        def f():
            for half in range(2):
                pz, pzk = z_ring.next()
                for m in range(8):
                    mm(pz, mergedT[:, m, t * 128:(t + 1) * 128], Wo[:, m, half * 512:(half + 1) * 512],
                       m == 0, m == 7, reads=["Wo"] + mkeys, writes=[pzk])
                S.op("dve", lambda e, half=half, pz=pz: e.tensor_tensor(
                    x_mid[:, T, half * 512:(half + 1) * 512], pz, x_mid[:, T, half * 512:(half + 1) * 512], ALU.add),
                     reads=[pzk, "xm%d" % T], writes=["xm%d" % T])
        return f

    flatA = []
    for i in range(4):
        for k in range(4):
            flatA.append(gates_step(i, k))
        for t in range(4):
            flatA.append(z_tile(i, t))
    evA = {}

    def addA(pos, ev):
        evA.setdefault(pos, []).append(ev)

    for i in range(4):
        for t in range(4):
            if i >= 1:
                base = 8 * (i - 1)
                addA(base + t - 1, ("act1", i, t))
                addA(base + t, ("dve1", i, t))
                addA(base + t + 2, ("post1", i, t))
            base = 8 * i + 4 + t
            addA(base, ("act2", i, t))
            addA(base + 1, ("dve2", i, t))
            addA(base + 3, ("post2", i, t))
    stA = {}

    def do_evA(name, i, t):
        T = 4 * i + t
        if name == "ldx":
            load_x(T)
        elif name == "act1":
            stA[("n1", i, t)] = nsb_act(x_mid[:, T, :], "xm%d" % T, xb1_ring)
        elif name == "dve1":
            stA[("n1", i, t)] = nsb_dve(stA[("n1", i, t)])
        elif name == "post1":
            norm_post2(stA[("n1", i, t)], gmix, "gmix", xnT3[i % 2][:, :, t * 128:(t + 1) * 128],
                       ["x3_%d_%d" % (i, t)])
        elif name == "act2":
            stA[("n2", i, t)] = nsb_act(x_mid[:, T, :], "xm%d" % T, xb2_ring)
        elif name == "dve2":
            stA[("n2", i, t)] = nsb_dve(stA[("n2", i, t)])
        else:
            norm_post2(stA[("n2", i, t)], gmlp, "gmlp", xn2T_all[:, :, T * 128:(T + 1) * 128],
                       ["xn2T_%d" % T, "ya_pre%d" % i, "O_n%d" % i])

    def run_evA(j):
        evs = evA.get(j, [])
        for nm in ("ldx", "post1", "post2", "dve1", "dve2", "act1", "act2"):
            for (name, i, t) in evs:
                if name == nm:
                    do_evA(name, i, t)

    for i in range(1, 4):
        for t in range(4):
            if (i, t) != (1, 0):
                addA(max(8 * (i - 1) + t - 3, 0), ("ldx", i, t))
    for j, fn in enumerate(flatA):
        fn()
        if j == 0:
            for t in range(4):
                load_x(t, after=["Wg0", "Wg1"])
            load_x(4)
            run_evA(-1)
        run_evA(j)
    for j in range(len(flatA), len(flatA) + 4):
        run_evA(j)
    W1_pre = [(Wa.rearrange("p a b -> p (a b)").rearrange("p (c n) -> p c n", c=8), "Wa"),
              (Wb.rearrange("p a b -> p (a b)").rearrange("p (c n) -> p c n", c=8), "Wb")]
    for g_ in range(2):
        wload(W1_pre[g_][0], w_cols(w_1, 512 * g_, 512), W1_pre[g_][1])
    S.barrier()

    ar = Arena(big, B_BASE, TOP)
    h2g = [ar.take([128, 4, 2048], BF16) for _ in range(2)]
    W1_ring = Ring([W1_pre[0][0], W1_pre[1][0]], "W1", keys=["Wa", "Wb"])
    W2_ring = Ring([ar.take([128, 4, 1024], BF16) for _ in range(2)], "W2")
    r_ring = Ring([ar.take([128, 512], BF16) for _ in range(4)], "relu")
    xbf_ring = Ring([ar.take([128, 1024], BF16) for _ in range(2)], "xbf")
    mm_ring = Ring([bank(b) for b in range(0, 4)], "pmm", keys=["pb%d" % b for b in range(0, 4)])
    o_ring = Ring([bank(b) for b in range(4, 8)], "po", keys=["pb%d" % b for b in range(4, 8)])

    fin_pending = []

    def final_act(T):
        st, skey = stat_ring.next()
        xb, xbkey = xbf_ring.next()
        S.op("dve", lambda e: e.memset(st[:, 0:1], 0.0), writes=[skey])
        S.op("act", lambda e: e.activation(xb, x_mid[:, T, :], AF.Square, accum_out=st[:, 0:1]),
             reads=["xm%d" % T, skey], writes=[xbkey, skey])
        S.op("act", lambda e: e.activation(st[:, 1:2], st[:, 0:1], AF.Ln, bias=epsc[:, 0:1],
                                           scale=1.0 / D_MODEL), reads=[skey, "epsc"], writes=[skey])
        S.op("act", lambda e: e.activation(st[:, 2:3], st[:, 1:2], AF.Exp, scale=-0.5),
             reads=[skey], writes=[skey])
        return T, st, skey

    def final_dve_store(info):
        T, st, skey = info
        if T % 2:
            S.op("act", lambda e: e.activation(x_mid[:, T, :], x_mid[:, T, :], AF.Copy, scale=st[:, 2:3]),
                 reads=["xm%d" % T, skey], writes=["xm%d" % T])
            S.op("pool", lambda e: e.tensor_tensor(x_mid[:, T, :], x_mid[:, T, :], gfin, ALU.mult),
                 reads=["xm%d" % T, "gfin"], writes=["xm%d" % T])
        else:
            S.op("dve", lambda e: e.scalar_tensor_tensor(x_mid[:, T, :], x_mid[:, T, :], st[:, 2:3], gfin,
                                                         ALU.mult, ALU.mult),
                 reads=["xm%d" % T, skey, "gfin"], writes=["xm%d" % T])
        S.dma("sp", lambda e: e.dma_start(out=out_d[T * 128:(T + 1) * 128, :], in_=x_mid[:, T, :]),
              reads=["xm%d" % T])

    def mlp_in(g):
        w1t, w1k = W1_ring.next()
        if g >= 2:
            wload(w1t, w_cols(w_1, 512 * g, 512), w1k)
        hb = h2g[g % 2]
        for sl in range(4):
            xk2 = ["xn2T_%d" % (4 * sl + t) for t in range(4)]
            for fc in range(4):
                ph, phk = mm_ring.next()
                for c in range(8):
                    mm(ph, w1t[:, c, fc * 128:(fc + 1) * 128], xn2T_all[:, c, sl * 512:(sl + 1) * 512],
                       c == 0, c == 7, reads=[w1k] + xk2, writes=[phk])
                r, rk = r_ring.next()
                S.op("act", lambda e, r=r, ph=ph: e.activation(r, ph, AF.Relu), reads=[phk], writes=[rk])
                S.op("dve", lambda e, r=r, fc=fc, sl=sl: e.tensor_tensor(hb[:, fc, sl * 512:(sl + 1) * 512], r, r, ALU.mult),
                     reads=[rk], writes=["h2_%d_%d_%d" % (g % 2, fc, sl)])

    def mlp_out(g, final):
        w2t, w2k = W2_ring.next()
        wload(w2t, w_2[512 * g:512 * (g + 1), :].rearrange("(c p) n -> p c n", p=128), w2k)
        hb = h2g[g % 2]
        for T in range(16):
            for half in range(2):
                pz, pzk = o_ring.next()
                for fc in range(4):
                    mm(pz, hb[:, fc, T * 128:(T + 1) * 128], w2t[:, fc, half * 512:(half + 1) * 512],
                       fc == 0, fc == 3, reads=["h2_%d_%d_%d" % (g % 2, fc, T // 4), w2k], writes=[pzk])
                S.op("dve", lambda e, T=T, half=half, pz=pz: e.tensor_tensor(
                    x_mid[:, T, half * 512:(half + 1) * 512], pz, x_mid[:, T, half * 512:(half + 1) * 512], ALU.add),
                     reads=[pzk, "xm%d" % T], writes=["xm%d" % T])
            if final:
                fin_pending.append(final_act(T))
                if len(fin_pending) > 2:
                    final_dve_store(fin_pending.pop(0))
        if final:
            while fin_pending:
                final_dve_store(fin_pending.pop(0))

    mlp_in(0)
    for g in range(1, 8):
        mlp_in(g)
        mlp_out(g - 1, False)
    mlp_out(7, True)

    S.emit()
    return nc


def _core_layout(core):
    b = core // 2
    own = OWN_A if core % 2 == 0 else OWN_B
    other = OWN_B if core % 2 == 0 else OWN_A
    return b, own, other, own + other


def _bias_table(own, perm):
    part = np.arange(128, dtype=np.float64)
    tab = np.zeros((128, 4, 8, 4, 4), dtype=np.float64)
    for i in range(4):
        qmid = own[i] * 512 + 256
        for p in range(8):
            for j in range(4):
                kpos = perm[p] * 512 + j * 128 + part
                for h in range(4):
                    if perm[p] > own[i]:
                        tab[:, i, p, j, h] = NEG
                    else:
                        tab[:, i, p, j, h] = SLOPES[h] * (kpos - qmid)
    return tab.reshape(128, 512).astype(np.float32)


def make_in_maps(inputs):
    x = np.asarray(inputs["x"], dtype=np.float32)
    f = lambda a: np.ascontiguousarray(np.asarray(a, dtype=np.float32))
    w_in = f(inputs["w_in"][0])
    common = {
        "w_in": w_in,
        "w_a": f(inputs["w_a_out"][0]),
        "w_b": f(inputs["w_b_out"][0]),
        "w_o": f(inputs["w_o"][0]),
        "w_1": f(inputs["w_mlp_in"][0]),
        "w_2": f(inputs["w_mlp_out"][0]),
        "gmix": f(np.asarray(inputs["norm_mix_g"][0]).reshape(8, 128).T),
        "gmlp": f(np.asarray(inputs["norm_mlp_g"][0]).reshape(8, 128).T),
        "bgate": f(np.asarray(inputs["b_gate"][0]).reshape(16, 128).T),
        "convw": f(np.asarray(inputs["conv_w"][0]).reshape(3, 4, 128).transpose(2, 1, 0).reshape(128, 12)),
        "gsub": f(np.asarray(inputs["subln_g"][0]).reshape(128, 1)),
        "lam": f(np.concatenate([np.asarray(inputs[k][0]) for k in
                                 ("lambda_q1", "lambda_k1", "lambda_q2", "lambda_k2")]).reshape(1, 256)),
        "gfin": f(np.asarray(inputs["norm_final_g"]).reshape(1, 1024)),
        "ident": np.eye(128, dtype=np.float32).astype(ml_dtypes.bfloat16),
        "tri": np.where(np.arange(128)[:, None] > np.arange(128)[None, :], NEG, 0.0).astype(np.float32).astype(
            ml_dtypes.bfloat16),
    }
    maps = []
    for core in range(8):
        b, own, other, perm = _core_layout(core)
        xs = np.concatenate([x[b, nb * 512:(nb + 1) * 512] for nb in perm], axis=0)
        xh = np.zeros((8, 1024), dtype=np.float32)
        for i, nb in enumerate(own):
            if nb > 0:
                xh[2 * i:2 * i + 2] = x[b, nb * 512 - 2:nb * 512]
        m = dict(common)
        m["xs"] = np.ascontiguousarray(xs)
        m["xh"] = xh
        m["btab"] = _bias_table(own, perm)
        maps.append(m)
    return maps


_NC_CACHE = {}


def kernel(**inputs):
    if "nc" not in _NC_CACHE:
        _NC_CACHE["nc"] = build_program()
    nc = _NC_CACHE["nc"]
    in_maps = make_in_maps(inputs)
    res = run_bass_kernel_spmd(nc, in_maps, core_ids=list(range(8)))
    out = np.zeros((BATCH, SEQ, D_MODEL), dtype=np.float32)
    for core in range(8):
        b, own, other, perm = _core_layout(core)
        o = np.asarray(res.results[core]["out"], dtype=np.float32)
        for i, nb in enumerate(own):
            out[b, nb * 512:(nb + 1) * 512] = o[i * 512:(i + 1) * 512]
    return out
```
